# Optimizing a Trainium2 kernel written in Bass

```python
import math
import jax, jax.numpy as jnp
from jax import lax
import numpy as np

D_MODEL = 2048
BATCH = 2
SEQ = 16384
DEPTH = 4

N_MIXERS = 4
WIDTH = D_MODEL
EPS = 1e-6
CONV_WIDTH = 31
HG_HEAD_DIM = 128
HG_HEADS = WIDTH // HG_HEAD_DIM
HG_CHUNK = 64
DA_HEAD_DIM = 128
DA_HEADS = WIDTH // (2 * DA_HEAD_DIM)
DA_V_DIM = 2 * DA_HEAD_DIM
ROPE_THETA = 500000.0
ROPE_DIM = DA_HEAD_DIM // 4
Q_BLOCK = 128
FN_GROUPS = 8
FN_GROUP_DIM = WIDTH // FN_GROUPS
N_CONV = len(range(0, DEPTH, N_MIXERS))
N_HGRN = len(range(1, DEPTH, N_MIXERS))
N_DIFF = len(range(2, DEPTH, N_MIXERS))
N_FNET = len(range(3, DEPTH, N_MIXERS))

kernel_name = "hybrid_conv_hgrn2_diffattn_fnet_encoder"


def rms_norm(x, g):
    xf = x.astype(jnp.float32)
    y = xf * lax.rsqrt(jnp.mean(xf * xf, axis=-1, keepdims=True) + EPS)
    return (y * g.astype(jnp.float32)).astype(x.dtype)


def layer_norm(x, g, b):
    xf = x.astype(jnp.float32)
    mu = jnp.mean(xf, axis=-1, keepdims=True)
    var = jnp.mean(jnp.square(xf - mu), axis=-1, keepdims=True)
    y = (xf - mu) * lax.rsqrt(var + EPS)
    return (y * g.astype(jnp.float32) + b.astype(jnp.float32)).astype(x.dtype)


def conformer_conv(h, w_in, dw, dw_b, ln_g, ln_b, w_out):
    u = h @ w_in
    a, b, z = jnp.split(u, 3, axis=-1)
    v = a * jax.nn.sigmoid(b)
    pad = CONV_WIDTH // 2
    v = lax.conv_general_dilated(
        v, dw[:, None, :].astype(v.dtype), window_strides=(1,), padding=[(pad, pad)],
        dimension_numbers=("NWC", "WIO", "NWC"), feature_group_count=WIDTH) + dw_b
    v = jax.nn.silu(layer_norm(v, ln_g, ln_b))
    return (v * jax.nn.silu(z)) @ w_out


def hgrn_lower_bounds(table):
    lb = jnp.cumsum(jax.nn.softmax(table.astype(jnp.float32), axis=0), axis=0)
    return lb - lb[0:1]


def hgrn_direction(q, k, v, log_f):
    B, H, S, _ = q.shape
    nc = S // HG_CHUNK

    def chunks(t):
        return jnp.moveaxis(t.reshape(B, H, nc, HG_CHUNK, t.shape[-1]), 2, 0)

    lower_tri = jnp.tril(jnp.ones((HG_CHUNK, HG_CHUNK), dtype=bool))[:, :, None]

    def step(state, inp):
        qb, kb, vb, fb = inp
        b = jnp.cumsum(fb, axis=2)
        o_inter = jnp.einsum("bhtk,bhkv->bhtv", qb * jnp.exp(b), state)
        diff = b[:, :, :, None, :] - b[:, :, None, :, :]
        decay = jnp.exp(jnp.where(lower_tri, diff, -jnp.inf))
        scores = jnp.einsum("bhtk,bhsk,bhtsk->bhts", qb, kb, decay)
        o_intra = jnp.einsum("bhts,bhsv->bhtv", scores, vb)
        b_last = b[:, :, -1:, :]
        new_state = state * jnp.exp(b_last)[:, :, 0, :, None] + jnp.einsum(
            "bhsk,bhsv->bhkv", kb * jnp.exp(b_last - b), vb)
        return new_state, o_inter + o_intra

    state0 = jnp.zeros((B, H, q.shape[-1], v.shape[-1]), jnp.float32)
    _, out = lax.scan(step, state0, (chunks(q), chunks(k), chunks(v), chunks(log_f)))
    return jnp.moveaxis(out, 0, 2).reshape(B, H, S, v.shape[-1])


def hgrn2_mixer(h, w_in, lb_fwd, lb_bwd, o_norm, w_out):
    B, S, _ = h.shape
    u = h @ w_in
    q, a_f, a_b, i, z = jnp.split(u, 5, axis=-1)

    def heads(t):
        return t.astype(jnp.float32).reshape(B, S, HG_HEADS, HG_HEAD_DIM).transpose(0, 2, 1, 3)

    def forget(a, lb):
        a = a.astype(jnp.float32)
        log_f = jnp.logaddexp(jnp.log(lb), jnp.log1p(-lb) + jax.nn.log_sigmoid(a))
        key = (1.0 - lb) * jax.nn.sigmoid(-a)
        return heads(log_f), heads(key)

    qh, vh = heads(q), heads(i)
    lf_f, k_f = forget(a_f, lb_fwd)
    lf_b, k_b = forget(a_b, lb_bwd)
    o_fwd = hgrn_direction(qh, k_f, vh, lf_f)
    flip = lambda t: jnp.flip(t, axis=2)
    o_bwd = flip(hgrn_direction(flip(qh), flip(k_b), flip(vh), flip(lf_b)))
    o = rms_norm(o_fwd + o_bwd, o_norm)
    o = o.transpose(0, 2, 1, 3).reshape(B, S, WIDTH).astype(h.dtype)
    return (o * jax.nn.silu(z)) @ w_out


def rope_tables(positions):
    inv = 1.0 / (ROPE_THETA ** (jnp.arange(0, ROPE_DIM, 2, dtype=jnp.float32) / ROPE_DIM))
    ang = positions.astype(jnp.float32)[..., None] * inv
    return jnp.cos(ang), jnp.sin(ang)


def apply_partial_rope(x, cos, sin):
    xr, xp = x[..., :ROPE_DIM], x[..., ROPE_DIM:]
    x1, x2 = jnp.split(xr.astype(jnp.float32), 2, axis=-1)
    c = cos[:, :, None, None, :]
    s = sin[:, :, None, None, :]
    rot = jnp.concatenate([x1 * c - x2 * s, x2 * c + x1 * s], axis=-1)
    return jnp.concatenate([rot.astype(x.dtype), xp], axis=-1)


def diff_attention(h, cos, sin, w_in, q_norm, k_norm, lam_q1, lam_k1, lam_q2, lam_k2,
                   sub_norm, w_out, lam_init):
    B, S, _ = h.shape
    u = h @ w_in
    q, k, v, z = jnp.split(u, 4, axis=-1)
    q = rms_norm(q.reshape(B, S, DA_HEADS, 2, DA_HEAD_DIM), q_norm)
    k = rms_norm(k.reshape(B, S, DA_HEADS, 2, DA_HEAD_DIM), k_norm)
    q = apply_partial_rope(q, cos, sin).transpose(0, 2, 3, 1, 4)
    k = apply_partial_rope(k, cos, sin).transpose(0, 2, 3, 1, 4)
    v = v.reshape(B, S, DA_HEADS, DA_V_DIM).transpose(0, 2, 1, 3)
    f32 = jnp.float32
    lam = (jnp.exp(jnp.sum(lam_q1.astype(f32) * lam_k1.astype(f32)))
           - jnp.exp(jnp.sum(lam_q2.astype(f32) * lam_k2.astype(f32))) + lam_init)
    scale = 1.0 / math.sqrt(DA_HEAD_DIM)
    nq = S // Q_BLOCK
    qb_all = q.reshape(B, DA_HEADS, 2, nq, Q_BLOCK, DA_HEAD_DIM).transpose(3, 0, 1, 2, 4, 5)

    def block(qb):
        s = jnp.einsum("bhmqd,bhmkd->bhmqk", qb, k).astype(f32) * scale
        p = jax.nn.softmax(s, axis=-1)
        w = p[:, :, 0] - lam * p[:, :, 1]
        return jnp.einsum("bhqk,bhkv->bhqv", w.astype(v.dtype), v)

    o = lax.map(block, qb_all)
    o = o.transpose(1, 2, 0, 3, 4).reshape(B, DA_HEADS, S, DA_V_DIM)
    o = rms_norm(o, sub_norm) * (1.0 - lam_init)
    o = o.transpose(0, 2, 1, 3).reshape(B, S, WIDTH).astype(h.dtype)
    return (o * jax.nn.silu(z)) @ w_out


def fourier_mixer(h, w_in, group_w, w_out):
    B, S, _ = h.shape
    u = h @ w_in
    v, z = jnp.split(u, 2, axis=-1)
    vg = v.astype(jnp.float32).reshape(B, S, FN_GROUPS, FN_GROUP_DIM)
    f = jnp.fft.fftn(vg, axes=(1, 3), norm="ortho").real
    y = jnp.einsum("bsgc,gce->bsge", f.astype(h.dtype), group_w).reshape(B, S, WIDTH)
    return (y * jax.nn.silu(z)) @ w_out


def setup_inputs(seed: int = 0) -> dict:
    key = jax.random.key(seed)
    ks = iter(jax.random.split(key, 40))
    E, D = WIDTH, D_MODEL
    nrm = lambda shape, s: jax.random.normal(next(ks), shape, jnp.float32) * s
    gain = lambda shape: 1.0 + nrm(shape, 0.02)
    return {
        "x": nrm((BATCH, SEQ, D), 1.0),
        "positions": jnp.broadcast_to(jnp.arange(SEQ, dtype=jnp.int32)[None, :], (BATCH, SEQ)),
        "conv_norm": gain((N_CONV, D)),
        "conv_w_in": nrm((N_CONV, D, 3 * E), D ** -0.5),
        "conv_dw": nrm((N_CONV, CONV_WIDTH, E), CONV_WIDTH ** -0.5),
        "conv_dw_b": nrm((N_CONV, E), 0.02),
        "conv_ln_g": gain((N_CONV, E)),
        "conv_ln_b": nrm((N_CONV, E), 0.02),
        "conv_w_out": nrm((N_CONV, E, D), E ** -0.5),
        "hgrn_norm": gain((N_HGRN, D)),
        "hgrn_w_in": nrm((N_HGRN, D, 5 * E), D ** -0.5),
        "hgrn_lb_fwd": nrm((DEPTH, E), 0.1),
        "hgrn_lb_bwd": nrm((DEPTH, E), 0.1),
        "hgrn_o_norm": gain((N_HGRN, HG_HEAD_DIM)),
        "hgrn_w_out": nrm((N_HGRN, E, D), E ** -0.5),
        "diff_norm": gain((N_DIFF, D)),
        "diff_w_in": nrm((N_DIFF, D, 4 * E), D ** -0.5),
        "diff_q_norm": gain((N_DIFF, DA_HEAD_DIM)),
        "diff_k_norm": gain((N_DIFF, DA_HEAD_DIM)),
        "diff_lam_q1": nrm((N_DIFF, DA_HEAD_DIM), 0.1),
        "diff_lam_k1": nrm((N_DIFF, DA_HEAD_DIM), 0.1),
        "diff_lam_q2": nrm((N_DIFF, DA_HEAD_DIM), 0.1),
        "diff_lam_k2": nrm((N_DIFF, DA_HEAD_DIM), 0.1),
        "diff_sub_norm": gain((N_DIFF, DA_V_DIM)),
        "diff_w_out": nrm((N_DIFF, E, D), E ** -0.5),
        "fnet_norm": gain((N_FNET, D)),
        "fnet_w_in": nrm((N_FNET, D, 2 * E), D ** -0.5),
        "fnet_group_w": nrm((N_FNET, FN_GROUPS, FN_GROUP_DIM, FN_GROUP_DIM), FN_GROUP_DIM ** -0.5),
        "fnet_w_out": nrm((N_FNET, E, D), E ** -0.5),
    }


def reference(x, positions, conv_norm, conv_w_in, conv_dw, conv_dw_b, conv_ln_g, conv_ln_b, conv_w_out,
              hgrn_norm, hgrn_w_in, hgrn_lb_fwd, hgrn_lb_bwd, hgrn_o_norm, hgrn_w_out,
              diff_norm, diff_w_in, diff_q_norm, diff_k_norm, diff_lam_q1, diff_lam_k1, diff_lam_q2,
              diff_lam_k2, diff_sub_norm, diff_w_out,
              fnet_norm, fnet_w_in, fnet_group_w, fnet_w_out):
    cos, sin = rope_tables(positions)
    lb_fwd = hgrn_lower_bounds(hgrn_lb_fwd)
    lb_bwd = hgrn_lower_bounds(hgrn_lb_bwd)
    for layer in range(DEPTH):
        m, j = layer % N_MIXERS, layer // N_MIXERS
        if m == 0:
            h = rms_norm(x, conv_norm[j])
            y = conformer_conv(h, conv_w_in[j], conv_dw[j], conv_dw_b[j], conv_ln_g[j], conv_ln_b[j],
                               conv_w_out[j])
        elif m == 1:
            h = rms_norm(x, hgrn_norm[j])
            y = hgrn2_mixer(h, hgrn_w_in[j], lb_fwd[layer], lb_bwd[layer], hgrn_o_norm[j], hgrn_w_out[j])
        elif m == 2:
            h = rms_norm(x, diff_norm[j])
            lam_init = 0.8 - 0.6 * math.exp(-0.3 * layer)
            y = diff_attention(h, cos, sin, diff_w_in[j], diff_q_norm[j], diff_k_norm[j], diff_lam_q1[j],
                               diff_lam_k1[j], diff_lam_q2[j], diff_lam_k2[j], diff_sub_norm[j],
                               diff_w_out[j], lam_init)
        else:
            h = rms_norm(x, fnet_norm[j])
            y = fourier_mixer(h, fnet_w_in[j], fnet_group_w[j], fnet_w_out[j])
        x = x + y.astype(x.dtype)
    return x
```

```python
import math
from contextlib import ExitStack
import numpy as np
import concourse.bass as bass
import concourse.mybir as mybir
from concourse.bass_utils import run_bass_kernel_spmd

F32 = mybir.dt.float32
BF16 = mybir.dt.bfloat16
I32 = mybir.dt.int32
AF = mybir.ActivationFunctionType
ALU = mybir.AluOpType
AX = mybir.AxisListType

D = 2048
NCH = D // 128
EPS = 1e-6
NCORES = 8


class Buf:
    __slots__ = ("name", "w", "r")

    def __init__(self, name):
        self.name = name
        self.w = None
        self.r = []


class Op:
    __slots__ = ("eng", "fn", "deps", "dma", "chan", "val", "need", "done")


class Sched:
    ENG = ("pe", "act", "dve", "pool", "sp")

    def __init__(self, nc, es):
        self.nc = nc
        self.es = es
        self.ops = []
        self.esem = {e: es.enter_context(nc.semaphore("s_" + e)) for e in self.ENG}
        self.ecnt = {e: 0 for e in self.ENG}
        self.chan_sem = {}
        self.chan_cnt = {}
        self.chan_last = {}
        self.seen = {e: {} for e in self.ENG}
        self.engobj = {"pe": nc.tensor, "act": nc.scalar, "dve": nc.vector, "pool": nc.gpsimd, "sp": nc.sync}

    def _deps(self, op, reads, writes):
        deps = []
        for b in reads:
            if b.w is not None:
                deps.append((b.w, True))
        for b in writes:
            if b.w is not None:
                deps.append((b.w, False))
            for r in b.r:
                deps.append((r, False))
        out = {}
        for d, raw in deps:
            if d is op:
                continue
            if (not d.dma) and (not op.dma) and d.eng == op.eng:
                if op.eng == "pe" or not raw:
                    continue
            out[id(d)] = d
        for b in writes:
            b.w = op
            b.r = []
        for b in reads:
            if b.w is not op:
                b.r.append(op)
        return list(out.values())

    def op(self, eng, fn, reads=(), writes=()):
        o = Op()
        o.done = False
        o.eng, o.fn, o.dma, o.chan, o.val, o.need = eng, fn, False, None, None, False
        o.deps = self._deps(o, reads, writes)
        for d in o.deps:
            d.need = True
        self.ops.append(o)
        return o

    def dma(self, eng, fn, chan, reads=(), writes=()):
        o = Op()
        o.done = False
        o.eng, o.fn, o.dma, o.chan, o.val, o.need = eng, fn, True, chan, None, True
        o.deps = self._deps(o, reads, writes)
        prev = self.chan_last.get(chan)
        if prev is not None and all(prev is not d for d in o.deps):
            o.deps.append(prev)
        self.chan_last[chan] = o
        for d in o.deps:
            d.need = True
        if chan not in self.chan_sem:
            self.chan_sem[chan] = self.es.enter_context(self.nc.semaphore("c_%d" % len(self.chan_sem)))
            self.chan_cnt[chan] = 0
        self.ops.append(o)
        return o

    def barrier(self, bufs):
        pass

    def flush(self, final_wait=(), barrier=False):
        nc = self.nc
        queues = {e: [] for e in self.ENG}
        if barrier:
            last = {}
            for o in self.ops:
                if not o.dma:
                    last[o.eng] = o
            for o in last.values():
                o.need = True
        for o in self.ops:
            waits = []
            for d in o.deps:
                if getattr(d, "done", False):
                    continue
                if d.dma:
                    sem, val = self.chan_sem[d.chan], d.val
                    key = ("c", d.chan)
                else:
                    sem, val = self.esem[d.eng], d.val
                    key = ("e", d.eng)
                assert val is not None
                if self.seen[o.eng].get(key, 0) >= val:
                    continue
                self.seen[o.eng][key] = val
                waits.append((sem, val))
            if o.dma:
                self.chan_cnt[o.chan] += 16
                o.val = self.chan_cnt[o.chan]
                inc = (self.chan_sem[o.chan], 16)
            elif o.need:
                self.ecnt[o.eng] += 1
                o.val = self.ecnt[o.eng]
                inc = (self.esem[o.eng], 1)
            else:
                inc = None
            queues[o.eng].append((waits, o.fn, inc))
        fw = []
        for chan in final_wait:
            fw.append((self.chan_sem[chan], self.chan_cnt[chan]))
        allw = []
        if barrier:
            for en in self.ENG:
                if self.ecnt[en] > 0:
                    allw.append((("e", en), self.esem[en], self.ecnt[en]))
            for chan in self.chan_sem:
                if self.chan_cnt[chan] > 0:
                    allw.append((("c", chan), self.chan_sem[chan], self.chan_cnt[chan]))
            for en in self.ENG:
                for key, sem, val in allw:
                    self.seen[en][key] = val
            for o in self.ops:
                o.done = True
        self.ops = []

        def run(e, items, extra=()):
            for waits, fn, inc in items:
                for sem, val in waits:
                    e.wait_ge(sem, val)
                ins = fn(e)
                if inc is not None:
                    ins.then_inc(inc[0], inc[1])
            for sem, val in extra:
                e.wait_ge(sem, val)
            for key, sem, val in allw:
                e.wait_ge(sem, val)

        with nc.Block() as block:
            @block.tensor
            def _(e):
                run(e, queues["pe"])

            @block.scalar
            def _(e):
                run(e, queues["act"])

            @block.vector
            def _(e):
                run(e, queues["dve"])

            @block.gpsimd
            def _(e):
                run(e, queues["pool"])

            @block.sync
            def _(e):
                run(e, queues["sp"], fw)


class K:
    def __init__(self):
        self.nc = bass.Bass("TRN2", target_bir_lowering=False)
        self.es = ExitStack()
        self.s = Sched(self.nc, self.es)
        self.n = 0
        self.cur = self.es

    def dram_in(self, name, shape, dt=F32):
        return self.nc.dram_tensor(name, list(shape), dt, kind="ExternalInput").ap()

    def dram_out(self, name, shape, dt=F32):
        return self.nc.dram_tensor(name, list(shape), dt, kind="ExternalOutput").ap()

    def dram_tmp(self, name, shape, dt=F32):
        return self.nc.dram_tensor(name, list(shape), dt, kind="Internal").ap()

    def sb(self, shape, dt=F32, name=None, es=None):
        self.n += 1
        name = (name or "t") + "_%d" % self.n
        t = (es or self.cur).enter_context(self.nc.sbuf_tensor(name or "t%d" % self.n, list(shape), dt))
        return t

    def ps(self, shape, dt=F32, name=None, es=None):
        self.n += 1
        name = (name or "p") + "_%d" % self.n
        t = (es or self.cur).enter_context(self.nc.psum_tensor(name or "p%d" % self.n, list(shape), dt))
        return t


def act_recip_sqrt(k, s, out_sb, in_ps, scale, eps_t, rd, wr, tmp):
    s.op("act", lambda e: e.activation(out=tmp, in_=in_ps, func=AF.Sqrt, scale=scale, bias=eps_t), reads=rd, writes=wr[:1])
    s.op("dve", lambda e: e.reciprocal(out=out_sb, in_=tmp), reads=wr[:1], writes=wr[1:])


def build_outproj(T, conv_prologue=False, TT=512):
    TT = min(TT, T)
    k = K()
    nc, s = k.nc, k.s
    gT = k.dram_in("gT", [D, T], F32 if conv_prologue else BF16)
    xT = k.dram_in("xT", [D, T])
    w = k.dram_in("w", [D, D])
    if conv_prologue:
        szT = k.dram_in("szT", [D, T])
        lng = k.dram_in("lng", [128, NCH])
        lnb = k.dram_in("lnb", [128, NCH])
    yT = k.dram_out("yT", [D, T])
    nt = T // TT
    with k.es:
        wsb = k.sb([128, NCH, D], BF16, "wsb")
        Bw = Buf("w")
        wv = w.rearrange("(c p) o -> p c o", p=128)
        for c in range(NCH):
            s.dma("pool", lambda e, c=c: e.dma_start(out=wsb[:, c, :], in_=wv[:, c, :]), "w", writes=[Bw])
        ones = k.sb([128, 128], F32, "ones")
        Bones = Buf("ones")
        s.op("dve", lambda e: e.memset(ones[:], 1.0 / D), writes=[Bones])
        epst = k.sb([128, 1], F32, "eps")
        Beps = Buf("eps")
        s.op("dve", lambda e: e.memset(epst[:], EPS), writes=[Beps])
        if conv_prologue:
            lngs = k.sb([128, NCH], F32, "lng_s")
            lnbs = k.sb([128, NCH], F32, "lnb_s")
            Bln = Buf("ln")
            s.dma("sp", lambda e: e.dma_start(out=lngs[:], in_=lng[:, :]), "ln", writes=[Bln])
            s.dma("sp", lambda e: e.dma_start(out=lnbs[:], in_=lnb[:, :]), "ln", writes=[Bln])
        NB = 1 if conv_prologue else 2
        gin = [k.sb([128, NCH, TT], F32 if conv_prologue else BF16, "gin%d" % i) for i in range(NB)]
        Bgin = [Buf("gin%d" % i) for i in range(NB)]
        xin = [k.sb([128, NCH, TT], F32, "xin%d" % i) for i in range(NB)]
        Bxin = [Buf("xin%d" % i) for i in range(NB)]
        if conv_prologue:
            szin = [k.sb([128, NCH, TT], F32, "szin%d" % i) for i in range(NB)]
            Bsz = [Buf("sz%d" % i) for i in range(NB)]
            gbf = k.sb([128, NCH, TT], BF16, "gbf")
            Bgbf = Buf("gbf")
            sq = k.sb([128, TT], F32, "sq")
            Bsq = Buf("sq")
            mean = k.sb([128, TT], F32, "mean")
            Bmean = Buf("mean")
            rstd = k.sb([128, TT], F32, "rstd")
            Brstd = Buf("rstd")
            tmp = k.sb([128, TT], F32, "tmp")
            Btmp = Buf("tmp")
            tmp2 = k.sb([128, TT], F32, "tmp2")
            Btmp2 = Buf("tmp2")
            pstat = [k.ps([128, TT], F32, "pstat%d" % i) for i in range(2)]
            Bpstat = [Buf("pstat%d" % i) for i in range(2)]
        NP = 4
        pacc = [k.ps([128, TT], F32, "pacc%d" % i) for i in range(NP)]
        Bpacc = [Buf("pacc%d" % i) for i in range(NP)]
        yo = [k.sb([128, TT], F32, "yo%d" % i) for i in range(NP)]
        Byo = [Buf("yo%d" % i) for i in range(NP)]
        gTv = gT.rearrange("(c p) t -> p c t", p=128)
        xTv = xT.rearrange("(c p) t -> p c t", p=128)
        yTv = yT.rearrange("(c p) t -> p c t", p=128)
        if conv_prologue:
            szTv = szT.rearrange("(c p) t -> p c t", p=128)
        pi = 0
        for t in range(nt):
            b = t % NB
            tsl = slice(t * TT, (t + 1) * TT)
            if conv_prologue:
                for h in range(2):
                    cs = slice(h * 8, h * 8 + 8)
                    s.dma("sp", lambda e, b=b, tsl=tsl, cs=cs: e.dma_start(out=gin[b][:, cs, :], in_=gTv[:, cs, tsl]), "gin%d" % b, writes=[Bgin[b]])
                    s.dma("sp", lambda e, b=b, tsl=tsl, cs=cs: e.dma_start(out=szin[b][:, cs, :], in_=szTv[:, cs, tsl]), "sz%d" % b, writes=[Bsz[b]])
            else:
                for h in range(2):
                    cs = slice(h * 8, h * 8 + 8)
                    s.dma("sp", lambda e, b=b, tsl=tsl, cs=cs: e.dma_start(out=gin[b][:, cs, :], in_=gTv[:, cs, tsl]), "gin%d" % b, writes=[Bgin[b]])
            for h in range(2):
                cs = slice(h * 8, h * 8 + 8)
                s.dma("sp", lambda e, b=b, tsl=tsl, cs=cs: e.dma_start(out=xin[b][:, cs, :], in_=xTv[:, cs, tsl]), "xin%d" % b, writes=[Bxin[b]])
            if conv_prologue:
                for c in range(NCH):
                    s.op("pe", lambda e, c=c, b=b: e.matmul(pstat[0][:], ones[:], gin[b][:, c, :], start=(c == 0), stop=(c == NCH - 1)),
                         reads=[Bones, Bgin[b]], writes=[Bpstat[0]])
                for c in range(NCH):
                    s.op("act", lambda e, c=c, b=b: e.activation(out=sq[:], in_=gin[b][:, c, :], func=AF.Square), reads=[Bgin[b]], writes=[Bsq])
                    s.op("pe", lambda e, c=c: e.matmul(pstat[1][:], ones[:], sq[:], start=(c == 0), stop=(c == NCH - 1)),
                         reads=[Bones, Bsq], writes=[Bpstat[1]])
                s.op("dve", lambda e: e.tensor_copy(out=mean[:], in_=pstat[0][:]), reads=[Bpstat[0]], writes=[Bmean])
                s.op("dve", lambda e: e.tensor_tensor(out=tmp[:], in0=mean[:], in1=mean[:], op=ALU.mult), reads=[Bmean], writes=[Btmp])
                s.op("dve", lambda e: e.tensor_tensor(out=tmp2[:], in0=pstat[1][:], in1=tmp[:], op=ALU.subtract), reads=[Bpstat[1], Btmp], writes=[Btmp2])
                s.op("act", lambda e: e.activation(out=tmp[:], in_=tmp2[:], func=AF.Sqrt, scale=1.0, bias=epst[:]), reads=[Btmp2, Beps], writes=[Btmp])
                s.op("dve", lambda e: e.reciprocal(out=rstd[:], in_=tmp[:]), reads=[Btmp], writes=[Brstd])
                for c in range(NCH):
                    s.op("dve", lambda e, c=c, b=b: e.tensor_tensor(out=tmp2[:], in0=gin[b][:, c, :], in1=mean[:], op=ALU.subtract),
                         reads=[Bgin[b], Bmean], writes=[Btmp2])
                    s.op("pool", lambda e, c=c: e.tensor_tensor(out=tmp[:], in0=tmp2[:], in1=rstd[:], op=ALU.mult),
                         reads=[Btmp2, Brstd], writes=[Btmp])
                    s.op("act", lambda e, c=c: e.activation(out=sq[:], in_=tmp[:], func=AF.Silu, scale=lngs[:, c:c + 1], bias=lnbs[:, c:c + 1]),
                         reads=[Btmp, Bln], writes=[Bsq])
                    s.op("dve", lambda e, c=c, b=b: e.tensor_tensor(out=gbf[:, c, :], in0=sq[:], in1=szin[b][:, c, :], op=ALU.mult),
                         reads=[Bsq, Bsz[b]], writes=[Bgbf])
                gsrc, Bg = gbf, Bgbf
            else:
                gsrc, Bg = gin[b], Bgin[b]
            for oc in range(NCH):
                p = pi % NP
                pi += 1
                for c in range(NCH):
                    s.op("pe", lambda e, c=c, oc=oc, p=p, gsrc=gsrc: e.matmul(pacc[p][:], wsb[:, c, oc * 128:(oc + 1) * 128], gsrc[:, c, :],
                                                                              start=(c == 0), stop=(c == NCH - 1)),
                         reads=[Bw, Bg], writes=[Bpacc[p]])
                s.op("dve", lambda e, oc=oc, p=p, b=b: e.tensor_tensor(out=yo[p][:], in0=pacc[p][:], in1=xin[b][:, oc, :], op=ALU.add),
                     reads=[Bpacc[p], Bxin[b]], writes=[Byo[p]])
                s.dma("sp", lambda e, oc=oc, p=p, tsl=tsl: e.dma_start(out=yTv[:, oc, tsl], in_=yo[p][:]), "yo%d" % p, reads=[Byo[p]])
        s.flush(final_wait=["yo%d" % i for i in range(NP)])
    return nc


_CACHE = {}


def _get(key, fn):
    if key not in _CACHE:
        _CACHE[key] = fn()
    return _CACHE[key]


class InProj:
    def __init__(self, k, S, ncols, TT=512, wname="w", xbufs=2, lnexp=False):
        self.lnexp = lnexp
        self.k, self.S, self.TT, self.ncols = k, S, TT, ncols
        nc, s = k.nc, k.s
        self.xT = k.dram_in("xT", [D, S])
        self.w = k.dram_in(wname, [D, ncols])
        self.ng = k.dram_in("ng", [128, NCH])
        self.wsb = k.sb([128, NCH, ncols], BF16, "wsb")
        self.Bw = Buf("w")
        wv = self.w.rearrange("(c p) o -> p c o", p=128)
        for c in range(NCH):
            s.dma("pool", lambda e, c=c: e.dma_start(out=self.wsb[:, c, :], in_=wv[:, c, :]), "w", writes=[self.Bw])
        self.ngs = k.sb([128, NCH], F32, "ng_s")
        self.Bng = Buf("ng")
        s.dma("sp", lambda e: e.dma_start(out=self.ngs[:], in_=self.ng[:, :]), "ng", writes=[self.Bng])
        self.ones = k.sb([128, 128], BF16, "ones_b")
        self.Bones = Buf("ones")
        s.op("dve", lambda e: e.memset(self.ones[:], 1.0 / D), writes=[self.Bones])
        self.epst = k.sb([128, 1], F32, "eps")
        self.Beps = Buf("eps")
        s.op("dve", lambda e: e.memset(self.epst[:], EPS), writes=[self.Beps])
        self.NB = xbufs
        self.xin = [k.sb([128, NCH, TT], F32, "xin%d" % i) for i in range(self.NB)]
        self.Bxin = [Buf("xin%d" % i) for i in range(self.NB)]
        self.sq = [k.sb([128, TT], BF16, "sqb%d" % i) for i in range(2)]
        self.Bsq = [Buf("sq%d" % i) for i in range(2)]
        self.pss = k.ps([128, TT], F32, "pss")
        self.Bpss = Buf("pss")
        self.rt = k.sb([128, TT], F32, "rt")
        self.Brt = Buf("rt")
        self.rstd = k.sb([128, TT], F32, "rstd")
        self.Brstd = Buf("rstd")
        self.hT = k.sb([128, NCH, TT], BF16, "hT")
        self.BhT = Buf("hT")
        self.xTv = self.xT.rearrange("(c p) t -> p c t", p=128)

    def load(self, t):
        s, TT = self.k.s, self.TT
        b = t % self.NB
        tsl = slice(t * TT, (t + 1) * TT)
        for h in range(2):
            cs = slice(h * 8, h * 8 + 8)
            s.dma("sp", lambda e, b=b, tsl=tsl, cs=cs: e.dma_start(out=self.xin[b][:, cs, :], in_=self.xTv[:, cs, tsl]),
                  "xin%d" % b, writes=[self.Bxin[b]])

    def norm(self, t):
        s = self.k.s
        b = t % self.NB
        xin, Bx = self.xin[b], self.Bxin[b]
        for c in range(NCH):
            q = c % 2
            s.op("act", lambda e, c=c, q=q: e.activation(out=self.sq[q][:], in_=xin[:, c, :], func=AF.Square), reads=[Bx], writes=[self.Bsq[q]])
            s.op("pe", lambda e, c=c, q=q: e.matmul(self.pss[:], self.ones[:], self.sq[q][:], start=(c == 0), stop=(c == NCH - 1)),
                 reads=[self.Bones, self.Bsq[q]], writes=[self.Bpss])
        if self.lnexp:
            s.op("act", lambda e: e.activation(out=self.rt[:], in_=self.pss[:], func=AF.Ln, scale=1.0, bias=self.epst[:]),
                 reads=[self.Bpss, self.Beps], writes=[self.Brt])
            s.op("act", lambda e: e.activation(out=self.rstd[:], in_=self.rt[:], func=AF.Exp, scale=-0.5), reads=[self.Brt], writes=[self.Brstd])
        else:
            s.op("act", lambda e: e.activation(out=self.rt[:], in_=self.pss[:], func=AF.Sqrt, scale=1.0, bias=self.epst[:]),
                 reads=[self.Bpss, self.Beps], writes=[self.Brt])
            s.op("dve", lambda e: e.reciprocal(out=self.rstd[:], in_=self.rt[:]), reads=[self.Brt], writes=[self.Brstd])
        for c in range(NCH):
            s.op("dve", lambda e, c=c: e.scalar_tensor_tensor(out=self.hT[:, c, :], in0=xin[:, c, :], scalar=self.ngs[:, c:c + 1], in1=self.rstd[:],
                                                              op0=ALU.mult, op1=ALU.mult),
                 reads=[Bx, self.Bng, self.Brstd], writes=[self.BhT])

    def mm_fm(self, pt, Bpt, j, n=None):
        s = self.k.s
        for c in range(NCH):
            s.op("pe", lambda e, c=c: e.matmul(pt, self.wsb[:, c, j * 128:(j + 1) * 128], self.hT[:, c, :], start=(c == 0), stop=(c == NCH - 1)),
                 reads=[self.Bw, self.BhT], writes=[Bpt])

    def mm_tm(self, pt, Bpt, tok0, col0, ncol):
        s = self.k.s
        for c in range(NCH):
            s.op("pe", lambda e, c=c: e.matmul(pt, self.hT[:, c, tok0:tok0 + 128], self.wsb[:, c, col0:col0 + ncol], start=(c == 0), stop=(c == NCH - 1)),
                 reads=[self.Bw, self.BhT], writes=[Bpt])


def build_conv(S, TT=512):
    k = K()
    nc, s = k.nc, k.s
    CW = 31
    PAD = 15
    with k.es:
        ip = InProj(k, S, 1536, TT)
        dw = k.dram_in("dw", [128, 4, CW])
        dwb = k.dram_in("dwb", [128, 4])
        cvT = k.dram_out("cvT", [512, S])
        szT = k.dram_out("szT", [512, S])
        dws = k.sb([128, 4, CW], F32, "dws")
        dwbs = k.sb([128, 4], F32, "dwbs")
        Bdw = Buf("dw")
        s.dma("sp", lambda e: e.dma_start(out=dws[:], in_=dw[:, :, :]), "dw", writes=[Bdw])
        s.dma("sp", lambda e: e.dma_start(out=dwbs[:], in_=dwb[:, :]), "dw", writes=[Bdw])
        nt = S // TT
        W = TT + 2 * PAD
        vb = [[k.sb([128, W], F32, "vb%d_%d" % (q, i)) for i in range(2)] for q in range(4)]
        Bvb = [[Buf("vb%d_%d" % (q, i)) for i in range(2)] for q in range(4)]
        NP = 6
        pp = [k.ps([128, TT], F32, "pp%d" % i) for i in range(NP)]
        Bpp = [Buf("pp%d" % i) for i in range(NP)]
        sig = [k.sb([128, TT], F32, "sig%d" % i) for i in range(2)]
        Bsig = [Buf("sig%d" % i) for i in range(2)]
        szo = [k.sb([128, TT], F32, "szo%d" % i) for i in range(2)]
        Bszo = [Buf("szo%d" % i) for i in range(2)]
        acc = [k.sb([128, TT], F32, "acc%d" % i) for i in range(2)]
        Bacc = [Buf("acc%d" % i) for i in range(2)]
        cvv = cvT.rearrange("(c p) t -> p c t", p=128)
        szv = szT.rearrange("(c p) t -> p c t", p=128)
        pi = [0]

        def nextp():
            p = pi[0] % NP
            pi[0] += 1
            return p

        def conv_out(q, buf, t_out, ai):
            src, Bs = vb[q][buf], Bvb[q][buf]
            a, Ba = acc[ai], Bacc[ai]
            s.op("dve", lambda e: e.tensor_scalar(out=a[:], in0=src[:, 0:TT], scalar1=dws[:, q, 0:1], scalar2=dwbs[:, q:q + 1], op0=ALU.mult, op1=ALU.add),
                 reads=[Bs, Bdw], writes=[Ba])
            for kk in range(1, CW):
                s.op("dve", lambda e, kk=kk: e.scalar_tensor_tensor(out=a[:], in0=src[:, kk:kk + TT], scalar=dws[:, q, kk:kk + 1], in1=a[:],
                                                                    op0=ALU.mult, op1=ALU.add),
                     reads=[Bs, Bdw, Ba], writes=[Ba])
            tsl = slice(t_out * TT, (t_out + 1) * TT)
            s.dma("sp", lambda e: e.dma_start(out=cvv[:, q, tsl], in_=a[:]), "acc%d" % ai, reads=[Ba])

        ip.load(0)
        for t in range(nt):
            if t + 1 < nt:
                ip.load(t + 1)
            ip.norm(t)
            cur, prev = t % 2, 1 - (t % 2)
            tsl = slice(t * TT, (t + 1) * TT)
            for q in range(4):
                pa, pb, pz = nextp(), nextp(), nextp()
                ip.mm_fm(pp[pa][:], Bpp[pa], q)
                ip.mm_fm(pp[pb][:], Bpp[pb], 4 + q)
                ip.mm_fm(pp[pz][:], Bpp[pz], 8 + q)
                si = q % 2
                s.op("act", lambda e, pb=pb, si=si: e.activation(out=sig[si][:], in_=pp[pb][:], func=AF.Sigmoid), reads=[Bpp[pb]], writes=[Bsig[si]])
                s.op("act", lambda e, pz=pz, si=si: e.activation(out=szo[si][:], in_=pp[pz][:], func=AF.Silu), reads=[Bpp[pz]], writes=[Bszo[si]])
                s.dma("sp", lambda e, q=q, si=si, tsl=tsl: e.dma_start(out=szv[:, q, tsl], in_=szo[si][:]), "szo%d" % si, reads=[Bszo[si]])
                s.op("dve", lambda e, q=q, cur=cur, pa=pa, si=si: e.tensor_tensor(out=vb[q][cur][:, PAD:PAD + TT], in0=pp[pa][:], in1=sig[si][:], op=ALU.mult),
                     reads=[Bpp[pa], Bsig[si]], writes=[Bvb[q][cur]])
                if t == 0:
                    s.op("pool", lambda e, q=q, cur=cur: e.memset(vb[q][cur][:, 0:PAD], 0.0), writes=[Bvb[q][cur]])
                else:
                    s.op("pool", lambda e, q=q, cur=cur, prev=prev: e.tensor_copy(out=vb[q][cur][:, 0:PAD], in_=vb[q][prev][:, TT:TT + PAD]),
                         reads=[Bvb[q][prev]], writes=[Bvb[q][cur]])
                    s.op("pool", lambda e, q=q, cur=cur, prev=prev: e.tensor_copy(out=vb[q][prev][:, PAD + TT:W], in_=vb[q][cur][:, PAD:2 * PAD]),
                         reads=[Bvb[q][cur]], writes=[Bvb[q][prev]])
                    conv_out(q, prev, t - 1, q % 2)
                if t == nt - 1:
                    s.op("pool", lambda e, q=q, cur=cur: e.memset(vb[q][cur][:, PAD + TT:W], 0.0), writes=[Bvb[q][cur]])
                    conv_out(q, cur, t, q % 2)
        s.flush(final_wait=["acc0", "acc1", "szo0", "szo1"])
    return nc


def build_attn(S, lam_init, TT=512):
    k = K()
    nc, s = k.nc, k.s
    NQT = S // TT
    NKC = S // 128
    scale = 1.0 / math.sqrt(128.0)
    pos = k.dram_in("pos", [1, S], I32)
    ropec = k.dram_in("ropec", [128, 2])
    perm = k.dram_in("perm", [128, 128])
    qkg = k.dram_in("qkg", [128, 2])
    lamv = k.dram_in("lamv", [128, 4])
    subg = k.dram_in("subg", [128, 2])
    gT = k.dram_out("gT", [512, S], BF16)
    Qd = k.dram_tmp("Qd", [4, 128, S], BF16)
    Kd = k.dram_tmp("Kd", [4, 128, S], BF16)
    Vd = k.dram_tmp("Vd", [S, 512], BF16)
    Zd = k.dram_tmp("Zd", [512, S], F32)
    with k.es:
        ropes = k.sb([128, 2], F32, "ropes")
        perms = k.sb([128, 128], F32, "perms")
        qkgs = k.sb([128, 2], F32, "qkgs")
        lams = k.sb([128, 4], F32, "lams")
        subgs = k.sb([128, 2], F32, "subgs")
        Bc = Buf("consts")
        for dst, src in ((ropes, ropec), (perms, perm), (qkgs, qkg), (lams, lamv), (subgs, subg)):
            s.dma("sp", lambda e, dst=dst, src=src: e.dma_start(out=dst[:], in_=src[:, :]), "consts", writes=[Bc])
        onesf = k.sb([128, 128], F32, "onesf")
        Bof = Buf("onesf")
        s.op("dve", lambda e: e.memset(onesf[:], 1.0), writes=[Bof])
        onesb = k.sb([128, 128], BF16, "onesb")
        Bob = Buf("onesb")
        s.op("dve", lambda e: e.memset(onesb[:], 1.0), writes=[Bob])
        halfpi = k.sb([128, 1], F32, "halfpi")
        Bhp = Buf("halfpi")
        s.op("dve", lambda e: e.memset(halfpi[:], math.pi / 2), writes=[Bhp])
        eps2 = k.sb([128, 1], F32, "eps2")
        Be2 = Buf("eps2")
        s.op("dve", lambda e: e.memset(eps2[:], EPS), writes=[Be2])
        lp = k.sb([128, 2], F32, "lp")
        Blp = Buf("lp")
        s.op("dve", lambda e: e.tensor_tensor(out=lp[:, 0:1], in0=lams[:, 0:1], in1=lams[:, 1:2], op=ALU.mult), reads=[Bc], writes=[Blp])
        s.op("dve", lambda e: e.tensor_tensor(out=lp[:, 1:2], in0=lams[:, 2:3], in1=lams[:, 3:4], op=ALU.mult), reads=[Bc, Blp], writes=[Blp])
        nlam = k.sb([128, 1], F32, "nlam")
        Bnl = Buf("nlam")
        lex = k.sb([128, 2], F32, "lex")
        Blex = Buf("lex")
        with ExitStack() as es0:
            k.cur = es0
            plam = k.ps([128, 2], F32, "plam")
            Bpl0 = Buf("plam")
            s.op("pe", lambda e: e.matmul(plam[:], onesf[:], lp[:], start=True, stop=True), reads=[Bof, Blp], writes=[Bpl0])
            s.op("act", lambda e: e.activation(out=lex[:], in_=plam[:], func=AF.Exp), reads=[Bpl0], writes=[Blex])
            s.op("dve", lambda e: e.scalar_tensor_tensor(out=nlam[:], in0=lex[:, 1:2], scalar=-float(lam_init), in1=lex[:, 0:1], op0=ALU.add, op1=ALU.subtract),
                 reads=[Blex], writes=[Bnl])
            s.flush(barrier=True)
        with ExitStack() as esa:
            k.cur = esa
            ip = InProj(k, S, 2048, TT)
            posi = k.sb([128, TT], I32, "posi")
            Bposi = Buf("posi")
            ang = k.sb([128, TT], F32, "ang")
            Bang = Buf("ang")
            nn = k.sb([128, TT], F32, "nn")
            Bnn = Buf("nn")
            rr = k.sb([128, TT], F32, "rr")
            Brr = Buf("rr")
            ra = k.sb([128, TT], F32, "ra")
            Bra = Buf("ra")
            Cf = k.sb([128, TT], F32, "Cf")
            BCf = Buf("Cf")
            Sf = k.sb([128, TT], F32, "Sf")
            BSf = Buf("Sf")
            pq = [k.ps([128, TT], F32, "pq%d" % i) for i in range(2)]
            Bpq = [Buf("pq%d" % i) for i in range(2)]
            pss2 = k.ps([128, TT], F32, "pss2")
            Bpss2 = Buf("pss2")
            psw = k.ps([128, TT], F32, "psw")
            Bpsw = Buf("psw")
            pv = [k.ps([128, 512], F32, "pv%d" % i) for i in range(2)]
            Bpv = [Buf("pv%d" % i) for i in range(2)]
            qraw = k.sb([128, TT], F32, "qraw")
            Bqraw = Buf("qraw")
            qsq = k.sb([128, TT], F32, "qsq")
            Bqsq = Buf("qsq")
            rt2 = k.sb([128, TT], F32, "rt2")
            Brt2 = Buf("rt2")
            rs2 = k.sb([128, TT], F32, "rs2")
            Brs2 = Buf("rs2")
            qn = k.sb([128, TT], F32, "qn")
            Bqn = Buf("qn")
            t1 = k.sb([128, TT], F32, "t1")
            Bt1 = Buf("t1")
            t2 = k.sb([128, TT], F32, "t2")
            Bt2 = Buf("t2")
            qo = [k.sb([128, TT], BF16, "qo%d" % i) for i in range(2)]
            Bqo = [Buf("qo%d" % i) for i in range(2)]
            vo = [k.sb([128, 512], BF16, "vo%d" % i) for i in range(2)]
            Bvo = [Buf("vo%d" % i) for i in range(2)]
            zo = [k.sb([128, TT], F32, "zo%d" % i) for i in range(2)]
            Bzo = [Buf("zo%d" % i) for i in range(2)]
            BQd, BKd, BVd, BZd = Buf("Qd"), Buf("Kd"), Buf("Vd"), Buf("Zd")
            TWO_PI = 2.0 * math.pi
            C1 = 6.28125
            C2 = TWO_PI - C1
            MAGIC = 12582912.0
            PI_LO = 3.1415925
            ip.load(0)
            cnt = 0
            for t in range(NQT):
                if t + 1 < NQT:
                    ip.load(t + 1)
                ip.norm(t)
                tsl = slice(t * TT, (t + 1) * TT)
                s.dma("sp", lambda e, tsl=tsl: e.dma_start(out=posi[:], in_=pos[0, tsl].partition_broadcast(128)), "posi", writes=[Bposi])
                s.op("dve", lambda e: e.tensor_copy(out=ang[:], in_=posi[:]), reads=[Bposi], writes=[Bang])
                s.op("dve", lambda e: e.tensor_scalar(out=ang[:], in0=ang[:], scalar1=ropes[:, 0:1], scalar2=None, op0=ALU.mult), reads=[Bang, Bc], writes=[Bang])
                s.op("dve", lambda e: e.tensor_scalar(out=nn[:], in0=ang[:], scalar1=1.0 / TWO_PI, scalar2=MAGIC, op0=ALU.mult, op1=ALU.add), reads=[Bang], writes=[Bnn])
                s.op("dve", lambda e: e.tensor_scalar(out=nn[:], in0=nn[:], scalar1=-MAGIC, scalar2=None, op0=ALU.add), reads=[Bnn], writes=[Bnn])
                s.op("dve", lambda e: e.scalar_tensor_tensor(out=rr[:], in0=nn[:], scalar=-C1, in1=ang[:], op0=ALU.mult, op1=ALU.add), reads=[Bnn, Bang], writes=[Brr])
                s.op("dve", lambda e: e.scalar_tensor_tensor(out=rr[:], in0=nn[:], scalar=-C2, in1=rr[:], op0=ALU.mult, op1=ALU.add), reads=[Bnn, Brr], writes=[Brr])
                s.op("dve", lambda e: e.tensor_scalar(out=rr[:], in0=rr[:], scalar1=-PI_LO, scalar2=PI_LO, op0=ALU.max, op1=ALU.min), reads=[Brr], writes=[Brr])
                s.op("dve", lambda e: e.scalar_tensor_tensor(out=ra[:], in0=rr[:], scalar=-1.0, in1=rr[:], op0=ALU.mult, op1=ALU.max), reads=[Brr], writes=[Bra])
                s.op("act", lambda e: e.activation(out=Cf[:], in_=ra[:], func=AF.Sin, scale=-1.0, bias=halfpi[:]), reads=[Bra, Bhp], writes=[BCf])
                s.op("act", lambda e: e.activation(out=Sf[:], in_=rr[:], func=AF.Sin), reads=[Brr], writes=[BSf])
                s.op("dve", lambda e: e.tensor_scalar(out=Sf[:], in0=Sf[:], scalar1=ropes[:, 1:2], scalar2=None, op0=ALU.mult), reads=[BSf, Bc], writes=[BSf])
                for j in range(8):
                    p = cnt % 2
                    cnt += 1
                    isk = j >= 4
                    ip.mm_fm(pq[p][:], Bpq[p], j)
                    s.op("dve", lambda e, p=p: e.tensor_copy(out=qraw[:], in_=pq[p][:]), reads=[Bpq[p]], writes=[Bqraw])
                    s.op("act", lambda e: e.activation(out=qsq[:], in_=qraw[:], func=AF.Square), reads=[Bqraw], writes=[Bqsq])
                    s.op("pe", lambda e: e.matmul(pss2[:], onesf[:], qsq[:], start=True, stop=True), reads=[Bof, Bqsq], writes=[Bpss2])
                    s.op("act", lambda e: e.activation(out=rt2[:], in_=pss2[:], func=AF.Sqrt, scale=1.0 / 128.0, bias=eps2[:]), reads=[Bpss2, Be2], writes=[Brt2])
                    s.op("dve", lambda e: e.reciprocal(out=rs2[:], in_=rt2[:]), reads=[Brt2], writes=[Brs2])
                    gcol = 1 if isk else 0
                    s.op("dve", lambda e, gcol=gcol: e.scalar_tensor_tensor(out=qn[:], in0=qraw[:], scalar=qkgs[:, gcol:gcol + 1], in1=rs2[:], op0=ALU.mult, op1=ALU.mult),
                         reads=[Bqraw, Bc, Brs2], writes=[Bqn])
                    s.op("pe", lambda e: e.matmul(psw[:], perms[:], qn[:], start=True, stop=True), reads=[Bc, Bqn], writes=[Bpsw])
                    s.op("pool", lambda e: e.tensor_tensor(out=t1[:], in0=qn[:], in1=Cf[:], op=ALU.mult), reads=[Bqn, BCf], writes=[Bt1])
                    s.op("dve", lambda e: e.tensor_tensor(out=t2[:], in0=psw[:], in1=Sf[:], op=ALU.mult), reads=[Bpsw, BSf], writes=[Bt2])
                    s.op("pool", lambda e, p=p: e.tensor_tensor(out=qo[p][:], in0=t1[:], in1=t2[:], op=ALU.add), reads=[Bt1, Bt2], writes=[Bqo[p]])
                    dstd, Bd = (Kd, BKd) if isk else (Qd, BQd)
                    s.dma("sp", lambda e, p=p, dstd=dstd, jj=j % 4, tsl=tsl: e.dma_start(out=dstd[jj, :, tsl], in_=qo[p][:]), "qo%d" % p, reads=[Bqo[p]], writes=[Bd])
                for j in range(4):
                    p = cnt % 2
                    cnt += 1
                    ip.mm_fm(pq[p][:], Bpq[p], 12 + j)
                    s.op("act", lambda e, p=p: e.activation(out=zo[p][:], in_=pq[p][:], func=AF.Silu), reads=[Bpq[p]], writes=[Bzo[p]])
                    s.dma("sp", lambda e, p=p, j=j, tsl=tsl: e.dma_start(out=Zd[j * 128:(j + 1) * 128, tsl], in_=zo[p][:]), "zo%d" % p, reads=[Bzo[p]], writes=[BZd])
                for tb in range(TT // 128):
                    p = tb % 2
                    ip.mm_tm(pv[p][:], Bpv[p], tb * 128, 1024, 512)
                    s.op("act", lambda e, p=p: e.copy(out=vo[p][:], in_=pv[p][:]), reads=[Bpv[p]], writes=[Bvo[p]])
                    r0 = t * TT + tb * 128
                    s.dma("sp", lambda e, p=p, r0=r0: e.dma_start(out=Vd[r0:r0 + 128, :], in_=vo[p][:]), "vo%d" % p, reads=[Bvo[p]], writes=[BVd])
            s.flush(barrier=True)
        with ExitStack() as esb:
            k.cur = esb
            KT = k.sb([128, 2, S], BF16, "KT")
            BKT = Buf("KT")
            Vs = k.sb([128, NKC, 256], BF16, "Vs")
            BVs = Buf("Vs")
            QT = [k.sb([128, 2, TT], BF16, "QT%d" % i) for i in range(2)]
            BQT = [Buf("QT%d" % i) for i in range(2)]
            pS = [k.ps([128, TT], F32, "pS%d" % i) for i in range(2)]
            BpS = [Buf("pS%d" % i) for i in range(2)]
            pO = [[k.ps([128, TT], F32, "pO%d_%d" % (m, v)) for v in range(2)] for m in range(2)]
            BpO = [[Buf("pO%d_%d" % (m, v)) for v in range(2)] for m in range(2)]
            pL = [k.ps([128, TT], F32, "pL%d" % m) for m in range(2)]
            BpL = [Buf("pL%d" % m) for m in range(2)]
            NPT = 4
            pT = [k.sb([128, TT], BF16, "pT%d" % i) for i in range(NPT)]
            BpT = [Buf("pT%d" % i) for i in range(NPT)]
            rl = [k.sb([128, TT], F32, "rl%d" % m) for m in range(2)]
            Brl = [Buf("rl%d" % m) for m in range(2)]
            om = [k.sb([128, 2, TT], F32, "om%d" % m) for m in range(2)]
            Bom = [Buf("om%d" % m) for m in range(2)]
            od = k.sb([128, 2, TT], F32, "od")
            Bod = Buf("od")
            osq = k.sb([128, TT], F32, "osq")
            Bosq = Buf("osq")
            ort = k.sb([128, TT], F32, "ort")
            Bort = Buf("ort")
            ors = k.sb([128, TT], F32, "ors")
            Bors = Buf("ors")
            zs = [k.sb([128, 2, TT], F32, "zs%d" % i) for i in range(2)]
            Bzs = [Buf("zs%d" % i) for i in range(2)]
            go = [k.sb([128, 2, TT], BF16, "go%d" % i) for i in range(2)]
            Bgo = [Buf("go%d" % i) for i in range(2)]
            subs = k.sb([128, 2], F32, "subs")
            Bsubs = Buf("subs")
            s.op("dve", lambda e: e.tensor_scalar(out=subs[:], in0=subgs[:], scalar1=float(1.0 - lam_init), scalar2=None, op0=ALU.mult), reads=[Bc], writes=[Bsubs])
            Vdv = Vd.rearrange("(c p) v -> p c v", p=128)
            gTv = gT.rearrange("(c p) t -> p c t", p=128)
            Zdv = Zd.rearrange("(c p) t -> p c t", p=128)
            sc = 0
            pc = 0
            for hd in range(2):
                for m in range(2):
                    for h4 in range(4):
                        sl = slice(h4 * (S // 4), (h4 + 1) * (S // 4))
                        s.dma("sp", lambda e, m=m, hd=hd, sl=sl: e.dma_start(out=KT[:, m, sl], in_=Kd[hd * 2 + m, :, sl]), "KT", reads=[BKd], writes=[BKT])
                for h4 in range(4):
                    cs = slice(h4 * (NKC // 4), (h4 + 1) * (NKC // 4))
                    s.dma("sp", lambda e, hd=hd, cs=cs: e.dma_start(out=Vs[:, cs, :], in_=Vdv[:, cs, hd * 256:(hd + 1) * 256]), "Vs", reads=[BVd], writes=[BVs])
                for qt in range(NQT):
                    qb = qt % 2
                    tsl = slice(qt * TT, (qt + 1) * TT)
                    for m in range(2):
                        s.dma("sp", lambda e, m=m, hd=hd, qb=qb, tsl=tsl: e.dma_start(out=QT[qb][:, m, :], in_=Qd[hd * 2 + m, :, tsl]), "QT%d" % qb, reads=[BQd], writes=[BQT[qb]])
                    s.dma("sp", lambda e, hd=hd, qb=qb, tsl=tsl: e.dma_start(out=zs[qb][:], in_=Zdv[:, hd * 2:hd * 2 + 2, tsl]), "zs%d" % qb, reads=[BZd], writes=[Bzs[qb]])
                    for m in range(2):
                        for kc in range(NKC):
                            sp_ = sc % 2
                            sc += 1
                            pp_ = pc % NPT
                            pc += 1
                            s.op("pe", lambda e, m=m, kc=kc, sp_=sp_, qb=qb: e.matmul(pS[sp_][:], KT[:, m, kc * 128:(kc + 1) * 128], QT[qb][:, m, :], start=True, stop=True),
                                 reads=[BKT, BQT[qb]], writes=[BpS[sp_]])
                            s.op("act", lambda e, sp_=sp_, pp_=pp_: e.activation(out=pT[pp_][:], in_=pS[sp_][:], func=AF.Exp, scale=scale), reads=[BpS[sp_]], writes=[BpT[pp_]])
                            for vh in range(2):
                                s.op("pe", lambda e, m=m, kc=kc, vh=vh, pp_=pp_: e.matmul(pO[m][vh][:], Vs[:, kc, vh * 128:(vh + 1) * 128], pT[pp_][:], start=(kc == 0), stop=(kc == NKC - 1)),
                                     reads=[BVs, BpT[pp_]], writes=[BpO[m][vh]])
                            s.op("pe", lambda e, m=m, kc=kc, pp_=pp_: e.matmul(pL[m][:], onesb[:], pT[pp_][:], start=(kc == 0), stop=(kc == NKC - 1)),
                                 reads=[Bob, BpT[pp_]], writes=[BpL[m]])
                        s.op("dve", lambda e, m=m: e.reciprocal(out=rl[m][:], in_=pL[m][:]), reads=[BpL[m]], writes=[Brl[m]])
                        for vh in range(2):
                            s.op("dve", lambda e, m=m, vh=vh: e.tensor_tensor(out=om[m][:, vh, :], in0=pO[m][vh][:], in1=rl[m][:], op=ALU.mult),
                                 reads=[BpO[m][vh], Brl[m]], writes=[Bom[m]])
                    for vh in range(2):
                        s.op("dve", lambda e, vh=vh: e.scalar_tensor_tensor(out=od[:, vh, :], in0=om[1][:, vh, :], scalar=nlam[:, 0:1], in1=om[0][:, vh, :], op0=ALU.mult, op1=ALU.add),
                             reads=[Bom[0], Bom[1], Bnl], writes=[Bod])
                    for vh in range(2):
                        s.op("act", lambda e, vh=vh: e.activation(out=osq[:], in_=od[:, vh, :], func=AF.Square), reads=[Bod], writes=[Bosq])
                        sp_ = sc % 2
                        if vh == 0:
                            spn = sp_
                        s.op("pe", lambda e, vh=vh, spn=spn: e.matmul(pS[spn][:], onesf[:], osq[:], start=(vh == 0), stop=(vh == 1)), reads=[Bof, Bosq], writes=[BpS[spn]])
                    sc += 1
                    s.op("act", lambda e, spn=spn: e.activation(out=ort[:], in_=pS[spn][:], func=AF.Sqrt, scale=1.0 / 256.0, bias=eps2[:]), reads=[BpS[spn], Be2], writes=[Bort])
                    s.op("dve", lambda e: e.reciprocal(out=ors[:], in_=ort[:]), reads=[Bort], writes=[Bors])
                    for vh in range(2):
                        s.op("dve", lambda e, vh=vh: e.scalar_tensor_tensor(out=od[:, vh, :], in0=od[:, vh, :], scalar=subs[:, vh:vh + 1], in1=ors[:], op0=ALU.mult, op1=ALU.mult),
                             reads=[Bod, Bsubs, Bors], writes=[Bod])
                        s.op("pool", lambda e, vh=vh, qb=qb: e.tensor_tensor(out=go[qb][:, vh, :], in0=od[:, vh, :], in1=zs[qb][:, vh, :], op=ALU.mult),
                             reads=[Bod, Bzs[qb]], writes=[Bgo[qb]])
                    s.dma("sp", lambda e, hd=hd, qb=qb, tsl=tsl: e.dma_start(out=gTv[:, hd * 2:hd * 2 + 2, tsl], in_=go[qb][:]), "go%d" % qb, reads=[Bgo[qb]])
            s.flush(final_wait=["go0", "go1"], barrier=True)
        k.cur = k.es
    return nc


def _pc(v, n):
    return np.ascontiguousarray(np.asarray(v, np.float32).reshape(n, 128).T)


def attn_core_inputs(x_b, pos_b, w_cols, inp):
    inv = 1.0 / (500000.0 ** (np.arange(0, 32, 2, dtype=np.float32) / 32.0))
    ropec = np.zeros((128, 2), np.float32)
    ropec[0:16, 0] = inv
    ropec[16:32, 0] = inv
    ropec[0:16, 1] = -1.0
    ropec[16:32, 1] = 1.0
    perm = np.eye(128, dtype=np.float32)
    perm[0:32, 0:32] = 0
    for m in range(16):
        perm[m + 16, m] = 1.0
        perm[m, m + 16] = 1.0
    return {
        "xT": None if x_b is None else np.ascontiguousarray(x_b.T), "w": np.ascontiguousarray(w_cols), "ng": _pc(inp["diff_norm"][0], 16),
        "pos": np.ascontiguousarray(pos_b.reshape(1, -1).astype(np.int32)), "ropec": ropec, "perm": perm,
        "qkg": np.stack([inp["diff_q_norm"][0], inp["diff_k_norm"][0]], axis=1).astype(np.float32),
        "lamv": np.stack([inp["diff_lam_q1"][0], inp["diff_lam_k1"][0], inp["diff_lam_q2"][0], inp["diff_lam_k2"][0]], axis=1).astype(np.float32),
        "subg": _pc(inp["diff_sub_norm"][0], 2),
    }


def build_hgrn(S, TT=512):
    k = K()
    nc, s = k.nc, k.s
    NT = S // TT
    C = 64
    NC8 = TT // C
    lbt = k.dram_in("lbt", [128, 2, 4, 4])
    ong = k.dram_in("ong", [128, 1])
    masks = k.dram_in("masks", [128, 2, C])
    smask = k.dram_in("smask", [128, TT])
    ident = k.dram_in("ident", [128, 128])
    gT = k.dram_out("gT", [512, S], BF16)
    Od = k.dram_tmp("Od", [512, S], F32)
    with k.es:
        ip = InProj(k, S, 2560, TT, lnexp=True, xbufs=1)
        lbs = k.sb([128, 2, 4, 4], F32, "lbs")
        ongs = k.sb([128, 1], F32, "ongs")
        mks = k.sb([128, 2, C], F32, "mks")
        smk = k.sb([128, TT], F32, "smk")
        idb = k.sb([128, 128], BF16, "idb")
        Bc = Buf("consts")
        s.dma("sp", lambda e: e.dma_start(out=lbs[:], in_=lbt[:, :, :, :]), "consts", writes=[Bc])
        s.dma("sp", lambda e: e.dma_start(out=ongs[:], in_=ong[:, :]), "consts", writes=[Bc])
        s.dma("sp", lambda e: e.dma_start(out=mks[:], in_=masks[:, :, :]), "consts", writes=[Bc])
        s.dma("sp", lambda e: e.dma_start(out=smk[:], in_=smask[:, :]), "consts", writes=[Bc])
        s.dma("pool", lambda e: e.dma_start(out=idb[:], in_=ident[:, :]), "identc", writes=[Bc])
        onesf = k.sb([128, 128], F32, "onesf")
        Bof = Buf("onesf")
        s.op("dve", lambda e: e.memset(onesf[:], 1.0 / 128.0), writes=[Bof])
        lbe = k.sb([128, 2, 4, 4], F32, "lbe")
        Blbe = Buf("lbe")
        s.op("act", lambda e: e.activation(out=lbe[:], in_=lbs[:], func=AF.Exp), reads=[Bc], writes=[Blbe])
        lsum = k.sb([128, 2, 4], F32, "lsum")
        Bls = Buf("lsum")
        s.op("dve", lambda e: e.tensor_tensor(out=lsum[:], in0=lbe[:, :, :, 0], in1=lbe[:, :, :, 1], op=ALU.add), reads=[Blbe], writes=[Bls])
        s.op("dve", lambda e: e.tensor_tensor(out=lsum[:], in0=lsum[:], in1=lbe[:, :, :, 2], op=ALU.add), reads=[Blbe, Bls], writes=[Bls])
        s.op("dve", lambda e: e.tensor_tensor(out=lsum[:], in0=lsum[:], in1=lbe[:, :, :, 3], op=ALU.add), reads=[Blbe, Bls], writes=[Bls])
        s.op("dve", lambda e: e.reciprocal(out=lsum[:], in_=lsum[:]), reads=[Bls], writes=[Bls])
        lb = k.sb([128, 2, 4], F32, "lb")
        omlb = k.sb([128, 2, 4], F32, "omlb")
        Blb = Buf("lb")
        s.op("dve", lambda e: e.tensor_tensor(out=lb[:], in0=lbe[:, :, :, 1], in1=lsum[:], op=ALU.mult), reads=[Blbe, Bls], writes=[Blb])
        s.op("dve", lambda e: e.tensor_scalar(out=omlb[:], in0=lb[:], scalar1=-1.0, scalar2=1.0, op0=ALU.mult, op1=ALU.add), reads=[Blb], writes=[Blb])
        vtm = k.sb([128, TT // 128, 512], BF16, "vtm")
        Bvtm = Buf("vtm")
        pv = k.ps([128, 512], F32, "pv")
        Bpv = Buf("pv")
        pq = k.ps([128, TT], F32, "pq")
        Bpq = Buf("pq")
        pa = k.ps([128, TT], F32, "pa")
        Bpa = Buf("pa")
        pmisc = k.ps([128, 512], F32, "pmisc")
        psc = [pmisc[:, i * C:(i + 1) * C] for i in range(2)]
        Bpsc = [Buf("psc%d" % i) for i in range(2)]
        pdS = [pmisc[:, 128 + i * 128:256 + i * 128] for i in range(2)]
        BpdS = [Buf("pdS%d" % i) for i in range(2)]
        ptrb = k.ps([128, 256], BF16, "ptrb")
        ptr = [ptrb[:, i * 128:(i + 1) * 128] for i in range(2)]
        Bptr = [Buf("ptr%d" % i) for i in range(2)]
        poTb = k.ps([128, 2 * C], F32, "poTb")
        poT = [poTb[:, i * C:(i + 1) * C] for i in range(2)]
        BpoT = [Buf("poT%d" % i) for i in range(2)]
        ta = k.sb([128, TT], F32, "ta")
        Bta = Buf("ta")
        tb_ = k.sb([128, TT], F32, "tb")
        Btb = Buf("tb")
        ff = k.sb([128, TT], F32, "ff")
        Bff = Buf("ff")
        lf = k.sb([128, TT], F32, "lf")
        Blf = Buf("lf")
        kk = k.sb([128, TT], F32, "kk")
        Bkk = Buf("kk")
        bb = k.sb([128, TT], F32, "bb")
        Bbb = Buf("bb")
        arg = k.sb([128, TT], F32, "arg")
        Barg = Buf("arg")
        Eq = k.sb([128, TT], F32, "Eq")
        BEq = Buf("Eq")
        Ek = k.sb([128, TT], F32, "Ek")
        BEk = Buf("Ek")
        qe = [k.sb([128, TT], BF16, "qe%d" % h) for h in range(4)]
        Bqe = [Buf("qe%d" % h) for h in range(4)]
        ke = [k.sb([128, TT], BF16, "ke%d" % h) for h in range(4)]
        Bke = [Buf("ke%d" % h) for h in range(4)]
        scl = [k.sb([128, 3, NC8], F32, "scl%d" % h) for h in range(4)]
        Bscl = [Buf("scl%d" % h) for h in range(4)]
        scx = k.sb([128, 3, NC8], F32, "scx")
        Bscx = Buf("scx")
        S32 = [k.sb([128, 128], F32, "S32_%d" % h) for h in range(4)]
        BS32 = [Buf("S32_%d" % h) for h in range(4)]
        Sq = [k.sb([128, 128], BF16, "Sq%d" % i) for i in range(2)]
        BSq = [Buf("Sq%d" % i) for i in range(2)]
        Stmp = [k.sb([128, 128], F32, "Stmp%d" % i) for i in range(2)]
        BStmp = [Buf("Stmp%d" % i) for i in range(2)]
        scs = [k.sb([128, C], BF16, "scs%d" % i) for i in range(2)]
        Bscs = [Buf("scs%d" % i) for i in range(2)]
        kes = [k.sb([128, 128], BF16, "kes%d" % i) for i in range(2)]
        Bkes = [Buf("kes%d" % i) for i in range(2)]
        otile = k.sb([128, 4, TT], F32, "otile")
        Bot = Buf("otile")
        ofw = k.sb([128, 4, TT], F32, "ofw")
        Bofw = Buf("ofw")
        szt = k.sb([128, 4, TT], F32, "szt")
        Bszt = Buf("szt")
        osq, Bosq = ta, Bta
        orl, Borl = tb_, Btb
        ors, Bors = ff, Bff
        og, Bog = lf, Blf
        go = k.sb([128, 4, TT], BF16, "go")
        Bgo = Buf("go")
        BOd = Buf("Od")
        Odv = Od.rearrange("(c p) t -> p c t", p=128)
        gTv = gT.rearrange("(c p) t -> p c t", p=128)
        MIDF, MIDB = 31, 32
        ci = [0]
        for dirn in range(2):
            for h in range(4):
                s.op("pool", lambda e, h=h: e.memset(S32[h][:], 0.0), writes=[BS32[h]])
            order = list(range(NT)) if dirn == 0 else list(range(NT - 1, -1, -1))
            for oi, t in enumerate(order):
                ip.load(t)
                ip.norm(t)
                tsl = slice(t * TT, (t + 1) * TT)
                for tb in range(TT // 128):
                    ip.mm_tm(pv[:], Bpv, tb * 128, 1536, 512)
                    s.op("act", lambda e, tb=tb: e.copy(out=vtm[:, tb, :], in_=pv[:]), reads=[Bpv], writes=[Bvtm])
                if dirn == 1:
                    s.dma("sp", lambda e, tsl=tsl: e.dma_start(out=ofw[:], in_=Odv[:, :, tsl]), "ofw", reads=[BOd], writes=[Bofw])
                    for h in range(4):
                        ip.mm_fm(pq[:], Bpq, 16 + h)
                        s.op("act", lambda e: e.activation(out=ta[:], in_=pq[:], func=AF.Exp, scale=-1.0), reads=[Bpq], writes=[Bta])
                        s.op("pool", lambda e: e.tensor_scalar(out=tb_[:], in0=ta[:], scalar1=1.0, scalar2=None, op0=ALU.add), reads=[Bta], writes=[Btb])
                        s.op("dve", lambda e: e.reciprocal(out=ta[:], in_=tb_[:]), reads=[Btb], writes=[Bta])
                        s.op("dve", lambda e, h=h: e.tensor_tensor(out=szt[:, h, :], in0=pq[:], in1=ta[:], op=ALU.mult), reads=[Bpq, Bta], writes=[Bszt])
                for h in range(4):
                    ip.mm_fm(pq[:], Bpq, h)
                    ip.mm_fm(pa[:], Bpa, (4 if dirn == 0 else 8) + h)
                    s.op("act", lambda e: e.activation(out=ta[:], in_=pa[:], func=AF.Exp, scale=-1.0), reads=[Bpa], writes=[Bta])
                    s.op("pool", lambda e: e.tensor_scalar(out=tb_[:], in0=ta[:], scalar1=1.0, scalar2=None, op0=ALU.add), reads=[Bta], writes=[Btb])
                    s.op("dve", lambda e: e.reciprocal(out=ta[:], in_=tb_[:]), reads=[Btb], writes=[Bta])
                    s.op("dve", lambda e, h=h, dirn=dirn: e.tensor_scalar(out=ff[:], in0=ta[:], scalar1=omlb[:, dirn, h:h + 1], scalar2=lb[:, dirn, h:h + 1], op0=ALU.mult, op1=ALU.add),
                         reads=[Bta, Blb], writes=[Bff])
                    s.op("act", lambda e: e.activation(out=lf[:], in_=ff[:], func=AF.Ln), reads=[Bff], writes=[Blf])
                    s.op("pool", lambda e: e.tensor_scalar(out=kk[:], in0=ff[:], scalar1=-1.0, scalar2=1.0, op0=ALU.mult, op1=ALU.add), reads=[Bff], writes=[Bkk])
                    s.op("dve", lambda e: e.tensor_tensor_scan(out=bb[:], data0=smk[:], data1=lf[:], initial=0.0, op0=ALU.mult, op1=ALU.add), reads=[Bc, Blf], writes=[Bbb])
                    b3 = bb[:].rearrange("p (c t) -> p c t", t=C)
                    a3 = arg[:].rearrange("p (c t) -> p c t", t=C)
                    sc = scl[h]
                    if dirn == 0:
                        s.op("dve", lambda e, b3=b3, a3=a3: e.tensor_tensor(out=a3, in0=b3, in1=b3[:, :, MIDF:MIDF + 1].to_broadcast([128, NC8, C]), op=ALU.subtract),
                             reads=[Bbb], writes=[Barg])
                        s.op("pool", lambda e, b3=b3: e.tensor_copy(out=scx[:, 0, :], in_=b3[:, :, MIDF]), reads=[Bbb], writes=[Bscx])
                        s.op("pool", lambda e, b3=b3: e.tensor_copy(out=scx[:, 1, :], in_=b3[:, :, C - 1]), reads=[Bbb], writes=[Bscx])
                        s.op("pool", lambda e, b3=b3: e.tensor_tensor(out=scx[:, 2, :], in0=b3[:, :, C - 1], in1=b3[:, :, MIDF], op=ALU.subtract), reads=[Bbb], writes=[Bscx])
                        s.op("act", lambda e: e.activation(out=Eq[:], in_=arg[:], func=AF.Exp), reads=[Barg], writes=[BEq])
                        s.op("act", lambda e: e.activation(out=Ek[:], in_=arg[:], func=AF.Exp, scale=-1.0), reads=[Barg], writes=[BEk])
                    else:
                        s.op("pool", lambda e: e.tensor_tensor(out=tb_[:], in0=bb[:], in1=lf[:], op=ALU.subtract), reads=[Bbb, Blf], writes=[Btb])
                        e3 = tb_[:].rearrange("p (c t) -> p c t", t=C)
                        s.op("dve", lambda e, e3=e3, a3=a3: e.tensor_tensor(out=a3, in0=e3, in1=e3[:, :, MIDB:MIDB + 1].to_broadcast([128, NC8, C]), op=ALU.subtract),
                             reads=[Btb], writes=[Barg])
                        s.op("pool", lambda e, b3=b3, e3=e3: e.tensor_tensor(out=scx[:, 0, :], in0=b3[:, :, C - 1], in1=e3[:, :, MIDB], op=ALU.subtract), reads=[Bbb, Btb], writes=[Bscx])
                        s.op("pool", lambda e, b3=b3: e.tensor_copy(out=scx[:, 1, :], in_=b3[:, :, C - 1]), reads=[Bbb], writes=[Bscx])
                        s.op("pool", lambda e, e3=e3: e.tensor_copy(out=scx[:, 2, :], in_=e3[:, :, MIDB]), reads=[Btb], writes=[Bscx])
                        s.op("act", lambda e: e.activation(out=Eq[:], in_=arg[:], func=AF.Exp, scale=-1.0), reads=[Barg], writes=[BEq])
                        s.op("act", lambda e: e.activation(out=Ek[:], in_=arg[:], func=AF.Exp), reads=[Barg], writes=[BEk])
                    s.op("act", lambda e, sc=sc: e.activation(out=sc[:], in_=scx[:], func=AF.Exp), reads=[Bscx], writes=[Bscl[h]])
                    s.op("dve", lambda e, h=h: e.tensor_tensor(out=qe[h][:], in0=pq[:], in1=Eq[:], op=ALU.mult), reads=[Bpq, BEq], writes=[Bqe[h]])
                    s.op("pool", lambda e, h=h: e.tensor_tensor(out=ke[h][:], in0=kk[:], in1=Ek[:], op=ALU.mult), reads=[Bkk, BEk], writes=[Bke[h]])
                blks = list(range(TT // 128)) if dirn == 0 else list(range(TT // 128 - 1, -1, -1))
                for blk in blks:
                    bsl = slice(blk * 128, (blk + 1) * 128)
                    for h in range(4):
                        i2 = ci[0] % 2
                        ci[0] += 1
                        hc = slice(h * 128, (h + 1) * 128)
                        for sub in range(2):
                            csl = slice(blk * 128 + sub * C, blk * 128 + (sub + 1) * C)
                            s.op("pe", lambda e, h=h, i2=i2, sub=sub, csl=csl: e.matmul(psc[i2][sub * C:(sub + 1) * C, :], ke[h][:, csl], qe[h][:, csl], start=True, stop=True),
                                 reads=[Bke[h], Bqe[h]], writes=[Bpsc[i2]])
                        s.op("dve", lambda e, i2=i2, dirn=dirn: e.tensor_tensor(out=scs[i2][:], in0=psc[i2], in1=mks[:, dirn, :], op=ALU.mult),
                             reads=[Bpsc[i2], Bc], writes=[Bscs[i2]])
                        s.op("pe", lambda e, h=h, i2=i2, bsl=bsl: e.transpose(ptr[i2], ke[h][:, bsl], idb[:]), reads=[Bke[h], Bc], writes=[Bptr[i2]])
                        s.op("act", lambda e, i2=i2: e.copy(out=kes[i2][:], in_=ptr[i2]), reads=[Bptr[i2]], writes=[Bkes[i2]])
                        subs = (0, 1) if dirn == 0 else (1, 0)
                        for sub in subs:
                            c = blk * 2 + sub
                            r = slice(sub * C, (sub + 1) * C)
                            csl = slice(c * C, (c + 1) * C)
                            j2 = ci[0] % 2
                            ci[0] += 1
                            s.op("pool", lambda e, h=h, j2=j2, c=c: e.tensor_scalar(out=Sq[j2][:], in0=S32[h][:], scalar1=scl[h][:, 0, c:c + 1], scalar2=None, op0=ALU.mult),
                                 reads=[BS32[h], Bscl[h]], writes=[BSq[j2]])
                            s.op("pe", lambda e, h=h, j2=j2, csl=csl: e.matmul(poT[j2], Sq[j2][:], qe[h][:, csl], start=True, stop=False),
                                 reads=[BSq[j2], Bqe[h]], writes=[BpoT[j2]])
                            s.op("pe", lambda e, h=h, j2=j2, i2=i2, r=r, blk=blk, hc=hc: e.matmul(poT[j2], vtm[r, blk, hc], scs[i2][r, :], start=False, stop=True),
                                 reads=[Bvtm, Bscs[i2]], writes=[BpoT[j2]])
                            if dirn == 0:
                                s.op("act", lambda e, h=h, j2=j2, csl=csl: e.copy(out=otile[:, h, csl], in_=poT[j2]), reads=[BpoT[j2]], writes=[Bot])
                            else:
                                s.op("dve", lambda e, h=h, j2=j2, csl=csl: e.tensor_tensor(out=otile[:, h, csl], in0=poT[j2], in1=ofw[:, h, csl], op=ALU.add),
                                     reads=[BpoT[j2], Bofw], writes=[Bot])
                            s.op("pe", lambda e, j2=j2, i2=i2, r=r, blk=blk, hc=hc: e.matmul(pdS[j2], kes[i2][r, :], vtm[r, blk, hc], start=True, stop=True),
                                 reads=[Bkes[i2], Bvtm], writes=[BpdS[j2]])
                            s.op("pool", lambda e, h=h, j2=j2, c=c: e.tensor_scalar(out=Stmp[j2][:], in0=S32[h][:], scalar1=scl[h][:, 1, c:c + 1], scalar2=None, op0=ALU.mult),
                                 reads=[BS32[h], Bscl[h]], writes=[BStmp[j2]])
                            s.op("dve", lambda e, h=h, j2=j2, c=c: e.scalar_tensor_tensor(out=S32[h][:], in0=pdS[j2], scalar=scl[h][:, 2, c:c + 1], in1=Stmp[j2][:], op0=ALU.mult, op1=ALU.add),
                                 reads=[BpdS[j2], Bscl[h], BStmp[j2]], writes=[BS32[h]])
                if dirn == 0:
                    s.dma("sp", lambda e, tsl=tsl: e.dma_start(out=Odv[:, :, tsl], in_=otile[:]), "otile", reads=[Bot], writes=[BOd])
                else:
                    for h in range(4):
                        s.op("act", lambda e, h=h: e.activation(out=osq[:], in_=otile[:, h, :], func=AF.Square), reads=[Bot], writes=[Bosq])
                        s.op("pe", lambda e: e.matmul(pa[:], onesf[:], osq[:], start=True, stop=True), reads=[Bof, Bosq], writes=[Bpa])
                        s.op("act", lambda e: e.activation(out=orl[:], in_=pa[:], func=AF.Ln, bias=ip.epst[:]), reads=[Bpa, ip.Beps], writes=[Borl])
                        s.op("act", lambda e: e.activation(out=ors[:], in_=orl[:], func=AF.Exp, scale=-0.5), reads=[Borl], writes=[Bors])
                        s.op("dve", lambda e, h=h: e.scalar_tensor_tensor(out=og[:], in0=otile[:, h, :], scalar=ongs[:, 0:1], in1=ors[:], op0=ALU.mult, op1=ALU.mult),
                             reads=[Bot, Bc, Bors], writes=[Bog])
                        s.op("pool", lambda e, h=h: e.tensor_tensor(out=go[:, h, :], in0=og[:], in1=szt[:, h, :], op=ALU.mult), reads=[Bog, Bszt], writes=[Bgo])
                    s.dma("sp", lambda e, tsl=tsl: e.dma_start(out=gTv[:, :, tsl], in_=go[:]), "go", reads=[Bgo])
            if dirn == 0:
                s.flush(barrier=True)
        s.flush(final_wait=["go"], barrier=True)
    return nc


def hgrn_core_inputs(x_b, w_cols, inp, hsl):
    C = 64
    lbt = np.stack([inp["hgrn_lb_fwd"][:, hsl], inp["hgrn_lb_bwd"][:, hsl]], axis=0)
    lbt = lbt.reshape(2, 4, 4, 128).transpose(3, 0, 2, 1)
    mf = np.triu(np.ones((C, C), np.float32))
    mb = np.tril(np.ones((C, C), np.float32))
    masks = np.stack([np.concatenate([mf, mf], 0), np.concatenate([mb, mb], 0)], axis=1)
    smask = np.ones((128, 512), np.float32)
    smask[:, ::C] = 0.0
    return {
        "xT": None if x_b is None else np.ascontiguousarray(x_b.T), "w": np.ascontiguousarray(w_cols), "ng": _pc(inp["hgrn_norm"][0], 16),
        "lbt": np.ascontiguousarray(lbt.astype(np.float32)), "ong": np.ascontiguousarray(inp["hgrn_o_norm"][0].reshape(128, 1).astype(np.float32)),
        "masks": np.ascontiguousarray(masks), "smask": smask, "ident": np.eye(128, dtype=np.float32),
    }


def build_fnet(S, TT=512):
    k = K()
    nc, s = k.nc, k.s
    NT = S // TT
    N1 = 128
    assert S % 128 == 0
    NS1 = S // 128
    cs = k.dram_in("cs", [128, 2, 512])
    w1 = k.dram_in("w1", [NS1, 2, 2 * NS1])
    cw = k.dram_in("cw", [128, NS1, 128])
    sw = k.dram_in("sw", [128, NS1, 128])
    gw = k.dram_in("gw", [128, 2, 2, 256])
    gT = k.dram_out("gT", [512, S], BF16)
    Ad = k.dram_tmp("Ad", [2, S, 512], F32)
    Fd = k.dram_tmp("Fd", [2, 256, S], BF16)
    Zd = k.dram_tmp("Zd", [512, S], F32)
    BAd, BFd, BZd = Buf("Ad"), Buf("Fd"), Buf("Zd")
    with k.es:
        with ExitStack() as esa:
            k.cur = esa
            ip = InProj(k, S, 1024, TT)
            css = k.sb([128, 2, 512], BF16, "css")
            Bcs = Buf("cs")
            s.dma("pool", lambda e: e.dma_start(out=css[:], in_=cs[:, :, :]), "cs", writes=[Bcs])
            vT = k.sb([128, 4, TT], BF16, "vT")
            BvT = Buf("vT")
            pq = [k.ps([128, TT], F32, "pq%d" % i) for i in range(2)]
            Bpq = [Buf("pq%d" % i) for i in range(2)]
            pA = [k.ps([128, 512], F32, "pA%d" % i) for i in range(2)]
            BpA = [Buf("pA%d" % i) for i in range(2)]
            zo = [k.sb([128, TT], F32, "zo%d" % i) for i in range(2)]
            Bzo = [Buf("zo%d" % i) for i in range(2)]
            ao = [k.sb([128, 512], F32, "ao%d" % i) for i in range(2)]
            Bao = [Buf("ao%d" % i) for i in range(2)]
            cnt = 0
            ip.load(0)
            for t in range(NT):
                if t + 1 < NT:
                    ip.load(t + 1)
                ip.norm(t)
                tsl = slice(t * TT, (t + 1) * TT)
                for j in range(4):
                    p = cnt % 2
                    cnt += 1
                    ip.mm_fm(pq[p][:], Bpq[p], j)
                    s.op("act", lambda e, p=p, j=j: e.copy(out=vT[:, j, :], in_=pq[p][:]), reads=[Bpq[p]], writes=[BvT])
                for j in range(4):
                    p = cnt % 2
                    cnt += 1
                    ip.mm_fm(pq[p][:], Bpq[p], 4 + j)
                    s.op("act", lambda e, p=p: e.activation(out=zo[p][:], in_=pq[p][:], func=AF.Silu), reads=[Bpq[p]], writes=[Bzo[p]])
                    s.dma("sp", lambda e, p=p, j=j, tsl=tsl: e.dma_start(out=Zd[j * 128:(j + 1) * 128, tsl], in_=zo[p][:]), "zo%d" % p, reads=[Bzo[p]], writes=[BZd])
                for gl in range(2):
                    for tb in range(TT // 128):
                        p = cnt % 2
                        cnt += 1
                        for cc in range(2):
                            s.op("pe", lambda e, p=p, gl=gl, cc=cc, tb=tb: e.matmul(pA[p][:], vT[:, gl * 2 + cc, tb * 128:(tb + 1) * 128], css[:, cc, :], start=(cc == 0), stop=(cc == 1)),
                                 reads=[BvT, Bcs], writes=[BpA[p]])
                        s.op("dve", lambda e, p=p: e.tensor_copy(out=ao[p][:], in_=pA[p][:]), reads=[BpA[p]], writes=[Bao[p]])
                        r0 = t * TT + tb * 128
                        s.dma("sp", lambda e, p=p, gl=gl, r0=r0: e.dma_start(out=Ad[gl, r0:r0 + 128, :], in_=ao[p][:]), "ao%d" % p, reads=[Bao[p]], writes=[BAd])
            s.flush(barrier=True)
        with ExitStack() as esb:
            k.cur = esb
            W = 64
            w1s = k.sb([NS1, 2, 2 * NS1], F32, "w1s")
            cws = k.sb([128, NS1, 128], BF16, "cws")
            sws = k.sb([128, NS1, 128], BF16, "sws")
            Bw1 = Buf("w1")
            s.dma("sp", lambda e: e.dma_start(out=w1s[:], in_=w1[:, :, :]), "w1", writes=[Bw1])
            for h4 in range(4):
                ks = slice(h4 * (NS1 // 4), (h4 + 1) * (NS1 // 4))
                s.dma("pool", lambda e, ks=ks: e.dma_start(out=cws[:, ks, :], in_=cw[:, ks, :]), "cw", writes=[Bw1])
                s.dma("pool", lambda e, ks=ks: e.dma_start(out=sws[:, ks, :], in_=sw[:, ks, :]), "cw", writes=[Bw1])
            xs = [k.sb([NS1, 128, W], F32, "xs%d" % i) for i in range(2)]
            Bxs = [Buf("xs%d" % i) for i in range(2)]
            Yt = k.sb([128, W, 2 * NS1], BF16, "Yt")
            BYt = Buf("Yt")
            fT = k.sb([W, S], BF16, "fT")
            BfT = Buf("fT")
            pY = [k.ps([128, 2 * NS1], F32, "pY%d" % i) for i in range(2)]
            BpY = [Buf("pY%d" % i) for i in range(2)]
            pF = [k.ps([W, 4, 128], F32, "pF%d" % i) for i in range(2)]
            BpF = [Buf("pF%d" % i) for i in range(2)]
            cnt = 0
            for gl in range(2):
                Av = Ad[gl].rearrange("(a b) c -> a b c", b=128)
                for cb in range(256 // W):
                    for ri in range(2):
                        c0 = ri * 256 + cb * W
                        for h8 in range(8):
                            bs = slice(h8 * 16, (h8 + 1) * 16)
                            s.dma("sp", lambda e, ri=ri, bs=bs, c0=c0, Av=Av: e.dma_start(out=xs[ri][:, bs, :], in_=Av[:, bs, c0:c0 + W]), "xs%d" % ri,
                                  reads=[BAd], writes=[Bxs[ri]])
                    for c in range(W):
                        p = cnt % 2
                        cnt += 1
                        s.op("pe", lambda e, p=p, c=c: e.matmul(pY[p][:], xs[0][:, :, c], w1s[:, 0, :], start=True, stop=False), reads=[Bxs[0], Bw1], writes=[BpY[p]])
                        s.op("pe", lambda e, p=p, c=c: e.matmul(pY[p][:], xs[1][:, :, c], w1s[:, 1, :], start=False, stop=True), reads=[Bxs[1], Bw1], writes=[BpY[p]])
                        if c % 2 == 0:
                            s.op("act", lambda e, p=p, c=c: e.copy(out=Yt[:, c, :], in_=pY[p][:]), reads=[BpY[p]], writes=[BYt])
                        else:
                            s.op("dve", lambda e, p=p, c=c: e.tensor_copy(out=Yt[:, c, :], in_=pY[p][:]), reads=[BpY[p]], writes=[BYt])
                    fT3 = fT[:].rearrange("p (b a) -> p a b", a=NS1)
                    for k4 in range(NS1 // 4):
                        p = cnt % 2
                        cnt += 1
                        for kq in range(4):
                            k1 = k4 * 4 + kq
                            s.op("pe", lambda e, p=p, kq=kq, k1=k1: e.matmul(pF[p][:, kq, :], Yt[:, :, k1], cws[:, k1, :], start=True, stop=False), reads=[BYt, Bw1], writes=[BpF[p]])
                            s.op("pe", lambda e, p=p, kq=kq, k1=k1: e.matmul(pF[p][:, kq, :], Yt[:, :, NS1 + k1], sws[:, k1, :], start=False, stop=True), reads=[BYt, Bw1], writes=[BpF[p]])
                        if k4 % 2 == 0:
                            s.op("act", lambda e, p=p, k4=k4, fT3=fT3: e.copy(out=fT3[:, k4 * 4:(k4 + 1) * 4, :], in_=pF[p][:]), reads=[BpF[p]], writes=[BfT])
                        else:
                            s.op("dve", lambda e, p=p, k4=k4, fT3=fT3: e.tensor_copy(out=fT3[:, k4 * 4:(k4 + 1) * 4, :], in_=pF[p][:]), reads=[BpF[p]], writes=[BfT])
                    s.dma("sp", lambda e, gl=gl, cb=cb: e.dma_start(out=Fd[gl, cb * W:(cb + 1) * W, :], in_=fT[:]), "fT", reads=[BfT], writes=[BFd])
            s.flush(barrier=True)
        with ExitStack() as esc:
            k.cur = esc
            gws = k.sb([128, 2, 2, 256], BF16, "gws")
            Bgw = Buf("gw")
            s.dma("pool", lambda e: e.dma_start(out=gws[:], in_=gw[:, :, :, :]), "gw", writes=[Bgw])
            fin = [k.sb([128, 2, TT], BF16, "fin%d" % i) for i in range(2)]
            Bfin = [Buf("fin%d" % i) for i in range(2)]
            zin = [k.sb([128, 2, TT], F32, "zin%d" % i) for i in range(2)]
            Bzin = [Buf("zin%d" % i) for i in range(2)]
            go = [k.sb([128, 2, TT], BF16, "go%d" % i) for i in range(2)]
            Bgo = [Buf("go%d" % i) for i in range(2)]
            py = [k.ps([128, TT], F32, "py%d" % i) for i in range(2)]
            Bpy = [Buf("py%d" % i) for i in range(2)]
            Fdv = [Fd[gl].rearrange("(c p) t -> p c t", p=128) for gl in range(2)]
            Zdv = Zd.rearrange("(c p) t -> p c t", p=128)
            gTv = gT.rearrange("(c p) t -> p c t", p=128)
            cnt = 0
            it = 0
            for gl in range(2):
                for t in range(NT):
                    b = it % 2
                    it += 1
                    tsl = slice(t * TT, (t + 1) * TT)
                    s.dma("sp", lambda e, b=b, gl=gl, tsl=tsl: e.dma_start(out=fin[b][:], in_=Fdv[gl][:, :, tsl]), "fin%d" % b, reads=[BFd], writes=[Bfin[b]])
                    s.dma("sp", lambda e, b=b, gl=gl, tsl=tsl: e.dma_start(out=zin[b][:], in_=Zdv[:, gl * 2:gl * 2 + 2, tsl]), "zin%d" % b, reads=[BZd], writes=[Bzin[b]])
                    for ec in range(2):
                        p = cnt % 2
                        cnt += 1
                        for cc in range(2):
                            s.op("pe", lambda e, p=p, b=b, gl=gl, ec=ec, cc=cc: e.matmul(py[p][:], gws[:, gl, cc, ec * 128:(ec + 1) * 128], fin[b][:, cc, :], start=(cc == 0), stop=(cc == 1)),
                                 reads=[Bgw, Bfin[b]], writes=[Bpy[p]])
                        s.op("dve", lambda e, p=p, b=b, ec=ec: e.tensor_tensor(out=go[b][:, ec, :], in0=py[p][:], in1=zin[b][:, ec, :], op=ALU.mult),
                             reads=[Bpy[p], Bzin[b]], writes=[Bgo[b]])
                    s.dma("sp", lambda e, b=b, gl=gl, tsl=tsl: e.dma_start(out=gTv[:, gl * 2:gl * 2 + 2, tsl], in_=go[b][:]), "go%d" % b, reads=[Bgo[b]])
            s.flush(final_wait=["go0", "go1"], barrier=True)
        k.cur = k.es
    return nc


def fnet_consts(S):
    NS1 = S // 128
    c = np.arange(256, dtype=np.float64)
    th = 2 * np.pi * np.outer(c, c) / 256.0
    cs = np.concatenate([np.cos(th), -np.sin(th)], axis=1) / 16.0
    cs = cs.reshape(2, 128, 512).transpose(1, 0, 2)
    s1 = np.arange(NS1, dtype=np.float64)
    th1 = 2 * np.pi * np.outer(s1, s1) / NS1
    C1, S1 = np.cos(th1), np.sin(th1)
    w1 = np.stack([np.concatenate([C1, -S1], 1), np.concatenate([S1, C1], 1)], axis=1)
    s2 = np.arange(128, dtype=np.float64)
    kk = (np.arange(NS1)[:, None] + NS1 * np.arange(128)[None, :]).astype(np.float64)
    th3 = 2 * np.pi * s2[:, None, None] * kk[None, :, :] / S
    th3 = np.mod(th3, 2 * np.pi)
    cw = np.cos(th3) / np.sqrt(S)
    sw = np.sin(th3) / np.sqrt(S)
    f = lambda a: np.ascontiguousarray(a.astype(np.float32))
    return {"cs": f(cs), "w1": f(w1), "cw": f(cw), "sw": f(sw)}


def fnet_core_inputs(x_b, w_cols, inp, g0, consts):
    gw = inp["fnet_group_w"][0][g0:g0 + 2]
    gw = gw.reshape(2, 2, 128, 256).transpose(2, 0, 1, 3)
    d = {"xT": None if x_b is None else np.ascontiguousarray(x_b.T), "w": np.ascontiguousarray(w_cols), "ng": _pc(inp["fnet_norm"][0], 16),
         "gw": np.ascontiguousarray(gw.astype(np.float32))}
    d.update(consts)
    return d


def _run(nc, in_maps):
    res = run_bass_kernel_spmd(nc, in_maps, core_ids=list(range(NCORES)))
    return res.results


def _outproj(xT, gfull, w_out, S, conv=None):
    B = len(xT)
    TOK = S * B // NCORES
    per = S // TOK
    nc = _get(("out", TOK, conv is not None), lambda: build_outproj(TOK, conv_prologue=conv is not None))
    maps = []
    for c in range(NCORES):
        b, q = c // per, c % per
        sl = slice(q * TOK, (q + 1) * TOK)
        m = {"gT": np.ascontiguousarray(gfull[b][:, sl]), "xT": np.ascontiguousarray(xT[b][:, sl]), "w": w_out}
        if conv is not None:
            m["szT"] = np.ascontiguousarray(conv["sz"][b][:, sl])
            m["lng"] = conv["lng"]
            m["lnb"] = conv["lnb"]
        maps.append(m)
    res = _run(nc, maps)
    out = []
    for b in range(B):
        out.append(np.concatenate([res[b * per + q]["yT"] for q in range(per)], axis=1))
    return out


def kernel(**inp):
    import ml_dtypes
    inp = {k_: np.asarray(v) for k_, v in inp.items()}
    x = inp["x"]
    B, S, _ = x.shape
    stop = inp.pop("_stop", None)
    per = NCORES // B
    xT = [np.ascontiguousarray(x[b].T) for b in range(B)]
    E = 2048
    CW = E // per

    def blockcols(j, nblk):
        return np.concatenate([np.arange(bk * E + j * CW, bk * E + (j + 1) * CW) for bk in range(nblk)])

    nc = _get(("conv", S), lambda: build_conv(S))
    w_in = inp["conv_w_in"][0]
    maps = []
    for c in range(NCORES):
        b, j = c // per, c % per
        csl = slice(j * CW, (j + 1) * CW)
        dw = inp["conv_dw"][0][:, csl]
        maps.append({"xT": xT[b], "w": np.ascontiguousarray(w_in[:, blockcols(j, 3)]), "ng": _pc(inp["conv_norm"][0], 16),
                     "dw": np.ascontiguousarray(dw.T.reshape(4, 128, 31).transpose(1, 0, 2)), "dwb": _pc(inp["conv_dw_b"][0][csl], 4)})
    res = _run(nc, maps)
    cv = [np.concatenate([res[b * per + j]["cvT"] for j in range(per)], axis=0) for b in range(B)]
    sz = [np.concatenate([res[b * per + j]["szT"] for j in range(per)], axis=0) for b in range(B)]
    del res
    xT = _outproj(xT, cv, inp["conv_w_out"][0], S, conv={"sz": sz, "lng": _pc(inp["conv_ln_g"][0], 16), "lnb": _pc(inp["conv_ln_b"][0], 16)})
    del cv, sz
    if stop == 1:
        return xT
    nc = _get(("hgrn", S), lambda: build_hgrn(S))
    w_in = inp["hgrn_w_in"][0]
    maps = []
    for c in range(NCORES):
        b, j = c // per, c % per
        maps.append(_hgrn_map(xT[b], w_in[:, blockcols(j, 5)], inp, slice(j * CW, (j + 1) * CW)))
    res = _run(nc, maps)
    g = [np.concatenate([res[b * per + j]["gT"] for j in range(per)], axis=0) for b in range(B)]
    del res
    xT = _outproj(xT, g, inp["hgrn_w_out"][0], S)
    if stop == 2:
        return xT
    lam_init = 0.8 - 0.6 * math.exp(-0.3 * 2)
    nc = _get(("attn", S), lambda: build_attn(S, lam_init))
    w_in = inp["diff_w_in"][0]
    maps = []
    for c in range(NCORES):
        b, j = c // per, c % per
        hs = [2 * j, 2 * j + 1]
        cols = []
        for base in (0, E):
            for h in hs:
                for m in range(2):
                    cols.append(np.arange(base + h * 256 + m * 128, base + h * 256 + (m + 1) * 128))
        for base in (2 * E, 3 * E):
            for h in hs:
                cols.append(np.arange(base + h * 256, base + (h + 1) * 256))
        cols = np.concatenate(cols)
        m_ = attn_core_inputs(None, inp["positions"][b], w_in[:, cols], inp)
        m_["xT"] = xT[b]
        maps.append(m_)
    res = _run(nc, maps)
    g = [np.concatenate([res[b * per + j]["gT"] for j in range(per)], axis=0) for b in range(B)]
    del res
    xT = _outproj(xT, g, inp["diff_w_out"][0], S)
    if stop == 3:
        return xT
    nc = _get(("fnet", S), lambda: build_fnet(S))
    w_in = inp["fnet_w_in"][0]
    consts = fnet_consts(S)
    maps = []
    for c in range(NCORES):
        b, j = c // per, c % per
        m_ = fnet_core_inputs(None, w_in[:, blockcols(j, 2)], inp, 2 * j, consts)
        m_["xT"] = xT[b]
        maps.append(m_)
    res = _run(nc, maps)
    g = [np.concatenate([res[b * per + j]["gT"] for j in range(per)], axis=0) for b in range(B)]
    del res
    xT = _outproj(xT, g, inp["fnet_w_out"][0], S)
    out = np.stack([np.ascontiguousarray(xT[b].T) for b in range(B)], axis=0).astype(np.float32)
    return out


def _hgrn_map(xT_b, w_cols, inp, hsl):
    m = hgrn_core_inputs(None, w_cols, inp, hsl)
    m["xT"] = xT_b
    return m
```
